# Optimizing a Trainium2 kernel written in Bass

```python
import math
import numpy as np
import jax
import jax.numpy as jnp
from jax import lax

D_MODEL = 4096
BATCH = 4
SEQ = 4096
DEPTH = 4

GRID_W = 64
CTX_LEN = 256
HEAD_DIM = 128
ROPE_THETA = 10000.0
ROPE_SEC = HEAD_DIM // 2
EPS = 1e-6
NEG_INF = -1e30

A_HEADS = 8
A_KV_HEADS = 2
WINDOW = 128
A_BLOCK = WINDOW
B_HEADS = 8
B_CONV = 3
B_CHUNK = 64
C_HEADS = 8
C_KV_HEADS = 2
Q_BLOCK = 128

N_BRANCH = 3
BRANCH_W = 1024
D_FF = 3072
ADA_RANK = 256
N_MOD = 9

A_QW = A_HEADS * HEAD_DIM
A_KVW = A_KV_HEADS * HEAD_DIM
B_W = B_HEADS * HEAD_DIM
C_QW = C_HEADS * HEAD_DIM
C_KVW = C_KV_HEADS * HEAD_DIM
IN_SIZES = (A_QW, A_KVW, A_KVW, 3 * B_W, B_W, 4 * B_HEADS, C_QW, C_KVW, C_KVW, N_BRANCH * D_MODEL)
IN_WIDTH = A_QW + 2 * A_KVW + 4 * B_W + 4 * B_HEADS + C_QW + 2 * C_KVW + N_BRANCH * D_MODEL

kernel_name = 'hybrid_parallel_mixer_dit'


def rms_norm(x, g):
    xf = x.astype(jnp.float32)
    y = xf * lax.rsqrt(jnp.mean(jnp.square(xf), axis=-1, keepdims=True) + EPS)
    return (y * g.astype(jnp.float32)).astype(x.dtype)


def l2_norm(x):
    xf = x.astype(jnp.float32)
    return xf * lax.rsqrt(jnp.sum(jnp.square(xf), axis=-1, keepdims=True) + EPS)


def to_heads(x, n):
    return x.reshape(x.shape[:-1] + (n, HEAD_DIM))


def split_proj(p):
    idx = [int(i) for i in np.cumsum(IN_SIZES)[:-1]]
    return jnp.split(p, idx, axis=-1)


def axial_rope_tables(rows):
    t = jnp.arange(rows * GRID_W)
    row = (t // GRID_W).astype(jnp.float32)
    col = (t % GRID_W).astype(jnp.float32)
    inv = 1.0 / (ROPE_THETA ** (jnp.arange(0, ROPE_SEC, 2, dtype=jnp.float32) / ROPE_SEC))
    ang = jnp.stack([row[:, None] * inv, col[:, None] * inv], axis=1)
    return jnp.cos(ang), jnp.sin(ang)


def apply_rope(x, cos, sin):
    b, s, h, _ = x.shape
    xr = x.astype(jnp.float32).reshape(b, s, h, 2, 2, ROPE_SEC // 2)
    x1, x2 = xr[..., 0, :], xr[..., 1, :]
    cs, sn = cos[None, :, None], sin[None, :, None]
    out = jnp.stack([x1 * cs - x2 * sn, x2 * cs + x1 * sn], axis=-2)
    return out.reshape(x.shape).astype(x.dtype)


def swiglu(u, wgu, wd):
    gt, up = jnp.split(u @ wgu, 2, axis=-1)
    return (jax.nn.silu(gt) * up) @ wd


def half_ffn(h, m, g_pre, g_post, wgu, wd):
    shift, scale, gate = m
    u = rms_norm(h, g_pre) * (1.0 + scale) + shift
    return h + 0.5 * gate * rms_norm(swiglu(u, wgu, wd), g_post)


def dense_attention(q, k, v, sink):
    b, t, h, _ = q.shape
    kv = k.shape[2]
    g = h // kv
    qg = q.reshape(b, t, kv, g, HEAD_DIM)
    s = jnp.einsum('btkgd,bmkd->bkgtm', qg, k).astype(jnp.float32) * HEAD_DIM ** -0.5
    if sink is not None:
        sk = jnp.broadcast_to(sink.astype(jnp.float32).reshape(1, kv, g, 1, 1), s.shape[:-1] + (1,))
        p = jax.nn.softmax(jnp.concatenate([s, sk], axis=-1), axis=-1)[..., :-1]
    else:
        p = jax.nn.softmax(s, axis=-1)
    o = jnp.einsum('bkgtm,bmkd->btkgd', p.astype(v.dtype), v)
    return o.reshape(b, t, h * HEAD_DIM)


def windowed_attention(q, k, v, k_ctx, v_ctx, sink):
    b, s, h, _ = q.shape
    kv = k.shape[2]
    g = h // kv
    nb = s // A_BLOCK
    span = 3 * A_BLOCK
    scale = HEAD_DIM ** -0.5
    qg = q.reshape(b, nb, A_BLOCK, kv, g, HEAD_DIM)
    pad = ((0, 0), (A_BLOCK, A_BLOCK), (0, 0), (0, 0))
    kp, vp = jnp.pad(k, pad), jnp.pad(v, pad)
    kb = jnp.concatenate([kp[:, j * A_BLOCK:j * A_BLOCK + s].reshape(b, nb, A_BLOCK, kv, HEAD_DIM) for j in range(3)], axis=2)
    vb = jnp.concatenate([vp[:, j * A_BLOCK:j * A_BLOCK + s].reshape(b, nb, A_BLOCK, kv, HEAD_DIM) for j in range(3)], axis=2)
    s_loc = jnp.einsum('bnqkgd,bnskd->bnkgqs', qg, kb).astype(jnp.float32) * scale
    qi = jnp.arange(A_BLOCK)[:, None]
    si = jnp.arange(span)[None, :]
    band = jnp.abs(si - qi - A_BLOCK) <= WINDOW
    kpos = (jnp.arange(nb)[:, None] - 1) * A_BLOCK + jnp.arange(span)[None, :]
    valid = (kpos >= 0) & (kpos < s)
    mask = band[None] & valid[:, None, :]
    s_loc = jnp.where(mask[None, :, None, None], s_loc, NEG_INF)
    s_ctx = jnp.einsum('bnqkgd,blkd->bnkgql', qg, k_ctx).astype(jnp.float32) * scale
    sk = jnp.broadcast_to(sink.astype(jnp.float32).reshape(1, 1, kv, g, 1, 1), s_ctx.shape[:-1] + (1,))
    p = jax.nn.softmax(jnp.concatenate([s_ctx, s_loc, sk], axis=-1), axis=-1)
    n_ctx = k_ctx.shape[1]
    p_ctx = p[..., :n_ctx].astype(v.dtype)
    p_loc = p[..., n_ctx:n_ctx + span].astype(v.dtype)
    o = jnp.einsum('bnkgql,blkd->bnqkgd', p_ctx, v_ctx) + jnp.einsum('bnkgqs,bnskd->bnqkgd', p_loc, vb)
    return o.reshape(b, s, h * HEAD_DIM)


def block_dense_attention(q, k_all, v_all):
    b, s, h, _ = q.shape
    kv = k_all.shape[2]
    g = h // kv
    nb = s // Q_BLOCK
    qb = q.reshape(b, nb, Q_BLOCK, kv, g, HEAD_DIM).transpose(1, 0, 2, 3, 4, 5)

    def one_block(qi):
        sc = jnp.einsum('bqkgd,bmkd->bkgqm', qi, k_all).astype(jnp.float32) * HEAD_DIM ** -0.5
        p = jax.nn.softmax(sc, axis=-1)
        return jnp.einsum('bkgqm,bmkd->bqkgd', p.astype(v_all.dtype), v_all)

    o = lax.map(one_block, qb)
    return o.transpose(1, 0, 2, 3, 4, 5).reshape(b, s, h * HEAD_DIM)


def short_conv(x, w):
    t = x.shape[1]
    r = B_CONV // 2
    xp = jnp.pad(x, ((0, 0), (r, r), (0, 0)))
    y = sum(xp[:, j:j + t] * w[j] for j in range(B_CONV))
    return jax.nn.silu(y)


def gdn_prepare(qkv, gates, conv_w, A_log, dt_bias):
    qkv = short_conv(qkv, conv_w)
    q, k, v = jnp.split(qkv, 3, axis=-1)
    q = l2_norm(to_heads(q, B_HEADS)) * HEAD_DIM ** -0.5
    k = l2_norm(to_heads(k, B_HEADS))
    v = to_heads(v, B_HEADS).astype(jnp.float32)
    gts = gates.astype(jnp.float32).reshape(gates.shape[:-1] + (2, 2, B_HEADS))

    def dir_gates(d):
        g = -jnp.exp(A_log[d].astype(jnp.float32)) * jax.nn.softplus(gts[..., d, 0, :] + dt_bias[d].astype(jnp.float32))
        return g, jax.nn.sigmoid(gts[..., d, 1, :])

    return q, k, v, dir_gates(0), dir_gates(1)


def gdn_chunked(q, k, v, g, beta, state):
    b, t, h, dh = q.shape
    c = B_CHUNK
    n = t // c

    def chunks(a):
        a = a.reshape((b, n, c, h) + a.shape[3:])
        return jnp.moveaxis(a, 3, 1)

    qc, kc, vc = chunks(q), chunks(k), chunks(v)
    gc = jnp.cumsum(chunks(g), axis=-1)
    bc = chunks(beta)
    tri = jnp.tril(jnp.ones((c, c), dtype=bool))
    strict = jnp.tril(jnp.ones((c, c), dtype=bool), -1)
    diff = gc[..., :, None] - gc[..., None, :]
    decay = jnp.where(tri, jnp.exp(jnp.where(tri, diff, 0.0)), 0.0)
    kb = kc * bc[..., None]
    lmat = jnp.where(strict, jnp.einsum('bhncd,bhnsd->bhncs', kb, kc) * decay, 0.0)
    eye = jnp.eye(c, dtype=jnp.float32)
    tinv = lax.linalg.triangular_solve(lmat + eye, jnp.broadcast_to(eye, lmat.shape), left_side=True, lower=True)
    u = tinv @ (vc * bc[..., None])
    w = tinv @ (kb * jnp.exp(gc)[..., None])
    qk = jnp.einsum('bhncd,bhnsd->bhncs', qc, kc) * decay
    qd = qc * jnp.exp(gc)[..., None]
    kd = kc * jnp.exp(gc[..., -1:] - gc)[..., None]
    glast = jnp.exp(gc[..., -1])
    xs = tuple(jnp.moveaxis(a, 2, 0) for a in (u, w, qk, qd, kd, glast))

    def step(s_prev, inp):
        u_i, w_i, qk_i, qd_i, kd_i, gl_i = inp
        v_new = u_i - w_i @ s_prev
        o_i = qd_i @ s_prev + qk_i @ v_new
        s_next = s_prev * gl_i[..., None, None] + jnp.swapaxes(kd_i, -1, -2) @ v_new
        return s_next, o_i

    state, o = lax.scan(step, state, xs)
    o = jnp.moveaxis(jnp.moveaxis(o, 0, 2), 1, 3).reshape(b, t, h, dh)
    return o, state


def gdn_mixer(qkv_x, gate_x, qkv_c, gate_c, conv_w, A_log, dt_bias):
    qx, kx, vx, fwd_x, bwd_x = gdn_prepare(qkv_x, gate_x, conv_w, A_log, dt_bias)
    qc, kc, vc, fwd_c, bwd_c = gdn_prepare(qkv_c, gate_c, conv_w, A_log, dt_bias)
    s0 = jnp.zeros((qx.shape[0], B_HEADS, HEAD_DIM, HEAD_DIM), jnp.float32)

    def rev(a):
        return a[:, ::-1]

    oc_f, sc_f = gdn_chunked(qc, kc, vc, fwd_c[0], fwd_c[1], s0)
    oc_b, sc_b = gdn_chunked(rev(qc), rev(kc), rev(vc), rev(bwd_c[0]), rev(bwd_c[1]), s0)
    ox_f, _ = gdn_chunked(qx, kx, vx, fwd_x[0], fwd_x[1], sc_f)
    ox_b, _ = gdn_chunked(rev(qx), rev(kx), rev(vx), rev(bwd_x[0]), rev(bwd_x[1]), sc_b)
    return ox_f + rev(ox_b), oc_f + rev(oc_b)


def gated_head_norm(o, z, w):
    zh = to_heads(z, B_HEADS).astype(jnp.float32)
    y = o * lax.rsqrt(jnp.mean(jnp.square(o), axis=-1, keepdims=True) + EPS) * w.astype(jnp.float32) * jax.nn.silu(zh)
    return y.reshape(z.shape).astype(z.dtype)


def merge_branches(outs, gate_logits, w_br, w_out):
    gl = gate_logits.reshape(gate_logits.shape[:-1] + (N_BRANCH, D_MODEL))
    merged = sum(jax.nn.sigmoid(gl[..., i, :]) * (o @ w_br[i]) for i, o in enumerate(outs))
    return merged @ w_out


def token_mixer(u_x, u_c, w_in, a_sink, b_conv, b_A_log, b_dt_bias, b_norm, c_qnorm, c_knorm, w_br, w_out, cos, sin, need_ctx):
    aq, ak, av, bqkv, bz, bgate, cq, ck, cv, mgate = split_proj(u_x @ w_in)
    aqc, akc, avc, bqkvc, bzc, bgatec, cqc, ckc, cvc, mgatec = split_proj(u_c @ w_in)
    ka_c, va_c = to_heads(akc, A_KV_HEADS), to_heads(avc, A_KV_HEADS)
    o_a = windowed_attention(apply_rope(to_heads(aq, A_HEADS), cos, sin), apply_rope(to_heads(ak, A_KV_HEADS), cos, sin), to_heads(av, A_KV_HEADS), ka_c, va_c, a_sink)
    ob_x, ob_c = gdn_mixer(bqkv, bgate, bqkvc, bgatec, b_conv, b_A_log, b_dt_bias)
    o_b = gated_head_norm(ob_x, bz, b_norm)
    kc_c = rms_norm(to_heads(ckc, C_KV_HEADS), c_knorm)
    vc_c = to_heads(cvc, C_KV_HEADS)
    qc_x = apply_rope(rms_norm(to_heads(cq, C_HEADS), c_qnorm), cos, sin)
    kc_x = apply_rope(rms_norm(to_heads(ck, C_KV_HEADS), c_knorm), cos, sin)
    o_c = block_dense_attention(qc_x, jnp.concatenate([kc_c, kc_x], axis=1), jnp.concatenate([vc_c, to_heads(cv, C_KV_HEADS)], axis=1))
    y_x = merge_branches((o_a, o_b, o_c), mgate, w_br, w_out)
    if not need_ctx:
        return y_x, None
    oc_a = dense_attention(to_heads(aqc, A_HEADS), ka_c, va_c, a_sink)
    oc_b = gated_head_norm(ob_c, bzc, b_norm)
    oc_c = dense_attention(rms_norm(to_heads(cqc, C_HEADS), c_qnorm), kc_c, vc_c, None)
    y_c = merge_branches((oc_a, oc_b, oc_c), mgatec, w_br, w_out)
    return y_x, y_c


def setup_inputs(seed: int = 0) -> dict:
    key = jax.random.key(seed)
    ks = jax.random.split(key, 22)
    f32 = jnp.float32
    L = DEPTH

    def nrm(k, shape, scale):
        return jax.random.normal(k, shape, f32) * scale

    dt = jnp.exp(jax.random.uniform(ks[14], (L, 2, B_HEADS), f32, math.log(1e-3), math.log(1e-1)))
    return {
        'x': nrm(ks[0], (BATCH, SEQ, D_MODEL), 1.0),
        'c': nrm(ks[1], (BATCH, D_MODEL), 1.0),
        'ctx': nrm(ks[2], (BATCH, CTX_LEN, D_MODEL), 1.0),
        'c_ctx': nrm(ks[3], (D_MODEL,), 1.0),
        'ada_down': nrm(ks[4], (L, D_MODEL, ADA_RANK), D_MODEL ** -0.5),
        'ada_up': nrm(ks[5], (L, ADA_RANK, N_MOD * D_MODEL), 0.5 * ADA_RANK ** -0.5),
        'ada_bias': nrm(ks[6], (L, N_MOD * D_MODEL), 0.01),
        'norm_pre': 1.0 + nrm(ks[7], (L, 3, D_MODEL), 0.02),
        'norm_post': 1.0 + nrm(ks[8], (L, 3, D_MODEL), 0.02),
        'ffn_wgu': nrm(ks[9], (L, 2, D_MODEL, 2 * D_FF), D_MODEL ** -0.5),
        'ffn_wd': nrm(ks[10], (L, 2, D_FF, D_MODEL), D_FF ** -0.5),
        'w_in': nrm(ks[11], (L, D_MODEL, IN_WIDTH), D_MODEL ** -0.5),
        'a_sink': nrm(ks[12], (L, A_HEADS), 1.0),
        'b_conv': nrm(ks[13], (L, B_CONV, 3 * B_W), B_CONV ** -0.5),
        'b_A_log': jnp.log(jax.random.uniform(ks[15], (L, 2, B_HEADS), f32, 1.0, 16.0)),
        'b_dt_bias': dt + jnp.log(-jnp.expm1(-dt)),
        'b_norm': 1.0 + nrm(ks[16], (L, HEAD_DIM), 0.02),
        'c_qnorm': 1.0 + nrm(ks[17], (L, HEAD_DIM), 0.02),
        'c_knorm': 1.0 + nrm(ks[18], (L, HEAD_DIM), 0.02),
        'w_br': nrm(ks[19], (L, N_BRANCH, BRANCH_W, D_MODEL), BRANCH_W ** -0.5),
        'w_out': nrm(ks[20], (L, D_MODEL, D_MODEL), D_MODEL ** -0.5),
    }


def reference(x, c, ctx, c_ctx, ada_down, ada_up, ada_bias, norm_pre, norm_post, ffn_wgu, ffn_wd, w_in, a_sink, b_conv, b_A_log, b_dt_bias, b_norm, c_qnorm, c_knorm, w_br, w_out):
    rows = x.shape[1] // GRID_W
    cos, sin = axial_rope_tables(rows)
    h_x, h_c = x, ctx
    for l in range(DEPTH):
        need_ctx = l < DEPTH - 1
        mod_x = jnp.split(((jax.nn.silu(c) @ ada_down[l]) @ ada_up[l] + ada_bias[l])[:, None, :], N_MOD, axis=-1)
        mod_c = jnp.split(((jax.nn.silu(c_ctx)[None] @ ada_down[l]) @ ada_up[l] + ada_bias[l])[:, None, :], N_MOD, axis=-1)
        h_x = half_ffn(h_x, mod_x[0:3], norm_pre[l, 0], norm_post[l, 0], ffn_wgu[l, 0], ffn_wd[l, 0])
        h_c = half_ffn(h_c, mod_c[0:3], norm_pre[l, 0], norm_post[l, 0], ffn_wgu[l, 0], ffn_wd[l, 0])
        u_x = rms_norm(h_x, norm_pre[l, 1]) * (1.0 + mod_x[4]) + mod_x[3]
        u_c = rms_norm(h_c, norm_pre[l, 1]) * (1.0 + mod_c[4]) + mod_c[3]
        y_x, y_c = token_mixer(u_x, u_c, w_in[l], a_sink[l], b_conv[l], b_A_log[l], b_dt_bias[l], b_norm[l], c_qnorm[l], c_knorm[l], w_br[l], w_out[l], cos, sin, need_ctx)
        h_x = h_x + mod_x[5] * rms_norm(y_x, norm_post[l, 1])
        h_x = half_ffn(h_x, mod_x[6:9], norm_pre[l, 2], norm_post[l, 2], ffn_wgu[l, 1], ffn_wd[l, 1])
        if need_ctx:
            h_c = h_c + mod_c[5] * rms_norm(y_c, norm_post[l, 1])
            h_c = half_ffn(h_c, mod_c[6:9], norm_pre[l, 2], norm_post[l, 2], ffn_wgu[l, 1], ffn_wd[l, 1])
    return h_x
```

```python
import math
from contextlib import ExitStack
import numpy as np
import ml_dtypes
import concourse.bass as bass
import concourse.mybir as mybir
from concourse.bass_utils import run_bass_kernel_spmd

F32 = mybir.dt.float32
BF16 = mybir.dt.bfloat16
AF = mybir.ActivationFunctionType
ALU = mybir.AluOpType
AX = mybir.AxisListType
EPS = 1e-6
BIG = 1.0e4


class Cfg:
    def __init__(s, **kw):
        s.D = 4096; s.SEQ = 4096; s.CTX = 256; s.L = 4; s.GW = 64; s.H = 8; s.KV = 2
        s.DFF = 3072; s.RANK = 256; s.TT = 1024; s.AW = 52800; s.PERLAYER = False
        s.__dict__.update(kw)
        s.KC = s.D // 128; s.NT = s.CTX + s.SEQ; s.FC = s.DFF // 128; s.G = s.H // s.KV
        s.BW = s.H * 128; s.RC = s.RANK // 128
        o = 0
        s.o_aq = o; o += s.H * 128
        s.o_ak = o; o += s.KV * 128
        s.o_av = o; o += s.KV * 128
        s.o_bqkv = o; o += 3 * s.H * 128
        s.o_bz = o; o += s.H * 128
        s.o_bg = o; o += 4 * s.H
        s.o_cq = o; o += s.H * 128
        s.o_ck = o; o += s.KV * 128
        s.o_cv = o; o += s.KV * 128
        s.o_mg = o; o += 3 * s.D
        s.INW = o
        KC, H = s.KC, s.H
        s.v_npre = 0; s.v_npost = 3 * KC; s.v_abias = 6 * KC; s.v_bconv = 15 * KC
        s.v_bnorm = 15 * KC + 9 * H; s.v_cq = s.v_bnorm + 1; s.v_ck = s.v_bnorm + 2
        s.NV = s.v_bnorm + 3
        s.NR = 5 * H
        s.tiles = []
        for c0 in range(0, s.CTX, s.TT):
            s.tiles.append((c0, min(s.TT, s.CTX - c0), True))
        for c0 in range(0, s.SEQ, s.TT):
            s.tiles.append((s.CTX + c0, min(s.TT, s.SEQ - c0), False))


class Op:
    __slots__ = ("eng", "chan", "fn", "sig", "waits", "dma", "idx", "clock", "semval")


class Sched:
    ENGS = ("pe", "act", "dve", "pool", "sp")

    def __init__(s):
        s.streams = {e: [] for e in s.ENGS}
        s.clock = {e: {} for e in s.ENGS}
        s.nidx = {}
        s.lastw = {}
        s.readers = {}
        s.lastreal = {}
        s.dcount = {}

    NSUB = 8

    def emit(s, eng, chan, fn, reads, writes, dma):
        prev = None
        if dma:
            k = s.dcount.get(chan, 0); s.dcount[chan] = k + 1
            chan = "%s.%d" % (chan, k % s.NSUB)
            prev = s.lastreal.get(chan)
        op = Op(); op.eng = eng; op.chan = chan; op.fn = fn; op.sig = dma; op.dma = dma
        op.waits = []
        ck = s.clock[eng]

        def need(p, raw):
            if p is None:
                return
            if (not p.dma) and p.eng == eng and not raw:
                return
            if ck.get(p.chan, -1) >= p.idx:
                return
            op.waits.append(p); p.sig = True
            for c, i in p.clock.items():
                if ck.get(c, -1) < i:
                    ck[c] = i
            ck[p.chan] = p.idx

        need(prev, True)
        for k in reads:
            need(s.lastw.get(k), True)
            if k.startswith("ps"):
                rd = s.readers.get(k)
                if rd:
                    for r in rd.values():
                        if r.eng != eng:
                            need(r, False)
        for k in writes:
            need(s.lastw.get(k), False)
            rd = s.readers.get(k)
            if rd:
                for r in rd.values():
                    need(r, False)
        op.idx = s.nidx.get(chan, 0); s.nidx[chan] = op.idx + 1
        op.clock = dict(ck)
        for k in reads:
            s.readers.setdefault(k, {})[chan] = op
        for k in writes:
            s.lastw[k] = op; s.readers[k] = {}
        s.streams[eng].append(op)
        if fn is not None:
            s.lastreal[chan] = op
        return op

    def barrier(s):
        lasts = list(s.lastreal.values())
        for e in s.ENGS:
            op = Op(); op.eng = e; op.chan = e; op.fn = None; op.sig = False; op.dma = False; op.waits = []
            ck = s.clock[e]
            for p in lasts:
                if (not p.dma) and p.eng == e:
                    continue
                if ck.get(p.chan, -1) >= p.idx:
                    continue
                op.waits.append(p); p.sig = True
                for c, i in p.clock.items():
                    if ck.get(c, -1) < i:
                        ck[c] = i
                ck[p.chan] = p.idx
            op.idx = -1; op.clock = {}
            if op.waits:
                s.streams[e].append(op)

    def replay(s, nc):
        cnt = {}
        for e in s.ENGS:
            for op in s.streams[e]:
                if op.dma:
                    op.semval = 16 * (op.idx + 1)
                elif op.sig:
                    cnt[op.chan] = cnt.get(op.chan, 0) + 1
                    op.semval = cnt[op.chan]
        chans = sorted(s.nidx.keys())
        bname = {"pe": "tensor", "act": "scalar", "dve": "vector", "pool": "gpsimd", "sp": "sync"}
        with ExitStack() as es:
            sems = {c: es.enter_context(nc.semaphore("sem_" + c.replace(".", "_"))) for c in chans}
            block = es.enter_context(nc.Block())
            for e in s.ENGS:
                ops = s.streams[e]

                def body(eng, ops=ops):
                    for op in ops:
                        for p in op.waits:
                            eng.wait_ge(sems[p.chan], p.semval)
                        if op.fn is None:
                            continue
                        ins = op.fn(eng)
                        if op.sig:
                            ins.then_inc(sems[op.chan], 16 if op.dma else 1)

                getattr(block, bname[e])(body)


class Rot:
    def __init__(s, items):
        s.items = list(items); s.i = 0

    def __call__(s):
        x = s.items[s.i % len(s.items)]; s.i += 1
        return x


class Builder:
    def __init__(s, cfg, dbg=None):
        s.c = cfg
        s.dbg = dbg
        s.nc = bass.Bass("TRN2", target_bir_lowering=False)
        s.S = Sched()
        s.uid = 0
        s.dumped = set()
        s.marks = []
        c = cfg
        nc = s.nc
        L = c.L

        def inp(name, shape, dt=F32):
            return nc.dram_tensor(name, list(shape), dt, kind="ExternalInput").ap()

        def scr(name, shape, dt=F32):
            kind = "ExternalOutput" if (dbg and name in dbg) else "Internal"
            return nc.dram_tensor(name, list(shape), dt, kind=kind).ap()

        s.d_xT = inp("xT", [c.D, c.NT])
        s.d_cvec = inp("cvec", [128, c.KC * 2])
        s.d_vecs = inp("vecs", [L, 128, c.NV])
        s.d_rows = inp("rows", [L, 128, c.NR])
        s.d_cf = inp("cf32", [128, 3 * 128])
        s.d_maskA = inp("maskA", [128, 2 * c.G * 128], BF16)
        s.d_rope = inp("rope", [2, 128, c.SEQ])
        s.d_g64 = inp("g64", [64, 2 * 64 + 5 * c.H * 64])
        s.d_adown = inp("ada_down", [L, c.D, c.RANK])
        s.d_aup = inp("ada_up", [L, c.RANK, 9 * c.D])
        s.d_wgu = inp("ffn_wgu", [L, 2, c.D, 2 * c.DFF])
        s.d_wd = inp("ffn_wd", [L, 2, c.DFF, c.D])
        s.d_win = inp("w_in", [L, c.D, c.INW])
        s.d_wbr = inp("w_br", [L, 3, c.BW, c.D])
        s.d_wout = inp("w_out", [L, c.D, c.D])
        s.d_out = nc.dram_tensor("outT", [c.D, c.NT if c.PERLAYER else c.SEQ], F32, kind="ExternalOutput").ap()
        s.d_hT = scr("hT", [c.D, c.NT])
        s.d_yT = scr("yT", [c.D, c.NT])
        s.d_qa = scr("qa", [c.H, 128, c.NT], BF16)
        s.d_ka = scr("ka", [c.KV, 128, c.NT], BF16)
        s.d_va = scr("va", [c.NT, c.KV * 128], BF16)
        s.d_qc = scr("qc", [c.H, 128, c.NT], BF16)
        s.d_kc = scr("kc", [c.KV, 128, c.NT], BF16)
        s.d_vc = scr("vc", [c.NT, c.KV * 128], BF16)
        s.d_braw = scr("braw", [3 * c.H, 128, c.NT])
        s.d_zs = scr("zs", [c.H, 128, c.NT], BF16)
        s.d_gtok = scr("gtok", [c.NT, 4 * c.H])
        s.d_mg = scr("mg", [3, c.D, c.NT], BF16)
        s.d_oT = scr("oT", [3, c.H, 128, c.NT], BF16)
        s.d_gq = scr("gq", [c.H, 128, c.NT])
        s.d_gk = scr("gk", [c.H, 128, c.NT])
        s.d_gkt = scr("gkt", [c.NT, c.H * 128])
        s.d_gvt = scr("gvt", [c.NT, c.H * 128])
        s.d_of = scr("gof", [2, c.NT, c.H * 128])
        s.arena = nc.alloc_sbuf_tensor("arena", [128, c.AW], F32)
        s.aoff = 0
        s.ps = [nc.alloc_psum_tensor("ps%d" % i, [128, 512], F32) for i in range(8)]
        s.build()

    def buf(s, shape, dt=F32):
        n = int(np.prod(shape))
        words = n if dt == F32 else (n + 1) // 2
        words = (words + 7) // 8 * 8
        assert s.aoff + words <= s.c.AW, ("arena overflow", s.aoff, words)
        ap = s.arena[:, s.aoff:s.aoff + words]
        s.aoff += words
        if dt != F32:
            ap = ap.bitcast(dt)[:, 0:n]
        else:
            ap = ap[:, 0:n]
        if len(shape) == 2:
            ap = ap.rearrange("p (a b) -> p a b", a=shape[0])
        elif len(shape) == 3:
            ap = ap.rearrange("p (a b c) -> p a b c", a=shape[0], b=shape[1])
        return ap

    def dump(s, name, ap, keys, dt=F32):
        if not (s.dbg and s.dbg.get("dumps")):
            return
        if name in s.dumped:
            return
        s.dumped.add(name)
        d = s.nc.dram_tensor("dump_" + name, list(ap.shape), dt, kind="ExternalOutput").ap()
        s.dma("sp", "dump", d, ap, keys, ["dump_" + name])

    def reset(s, mark=None):
        s.S.barrier()
        if s.dbg is not None:
            s.marks.append({e: len(v) for e, v in s.S.streams.items()})
        s.aoff = s.base if mark is None else mark

    def key(s, base):
        s.uid += 1
        return "%s#%d" % (base, s.uid)

    def mm(s, out, lhsT, rhs, start, stop, reads, writes):
        return s.S.emit("pe", "pe", lambda e: e.matmul(out, lhsT, rhs, start=start, stop=stop),
                        reads, writes, False)

    def act(s, out, in_, func, reads, writes, bias=None, scale=1.0):
        if bias is None:
            fn = lambda e: e.activation(out=out, in_=in_, func=func, scale=scale)
        else:
            fn = lambda e: e.activation(out=out, in_=in_, func=func, bias=bias, scale=scale)
        return s.S.emit("act", "act", fn, reads, writes, False)

    def tt(s, eng, out, a, b, op, reads, writes):
        return s.S.emit(eng, eng, lambda e: e.tensor_tensor(out=out, in0=a, in1=b, op=op), reads, writes, False)

    def ts(s, eng, out, a, s1, op0, reads, writes, s2=None, op1=None):
        if op1 is None:
            fn = lambda e: e.tensor_scalar(out=out, in0=a, scalar1=s1, scalar2=None, op0=op0)
        else:
            fn = lambda e: e.tensor_scalar(out=out, in0=a, scalar1=s1, scalar2=s2, op0=op0, op1=op1)
        return s.S.emit(eng, eng, fn, reads, writes, False)

    def stt(s, out, in0, scalar, in1, op0, op1, reads, writes):
        return s.S.emit("dve", "dve", lambda e: e.scalar_tensor_tensor(out=out, in0=in0, scalar=scalar, in1=in1,
                                                                      op0=op0, op1=op1), reads, writes, False)

    def copy(s, eng, out, in_, reads, writes):
        if eng == "act":
            return s.act(out, in_, AF.Copy, reads, writes)
        return s.S.emit(eng, eng, lambda e: e.tensor_copy(out=out, in_=in_), reads, writes, False)

    def recip(s, out, in_, reads, writes):
        return s.S.emit("dve", "dve", lambda e: e.reciprocal(out=out, in_=in_), reads, writes, False)

    def memset(s, eng, ap, val, writes):
        return s.S.emit(eng, eng, lambda e: e.memset(ap, val), [], writes, False)

    def dma(s, q, chan, out, in_, reads, writes):
        return s.S.emit(q, chan, lambda e: e.dma_start(out=out, in_=in_), reads, writes, True)

    def build(s):
        c = s.c
        s.cf = s.buf([3, 128]); s.ident = s.cf[:, 0, :]; s.ones = s.cf[:, 1, :]; s.permR = s.cf[:, 2, :]
        s.onesb = s.buf([128], BF16)
        s.maskA = s.buf([2, c.G * 128], BF16)
        s.cvec = s.buf([c.KC, 2]); s.sc = s.buf([c.KC, 2])
        s.vec = s.buf([c.NV]); s.rowsb = s.buf([c.NR])
        s.mod = s.buf([9, c.KC, 2]); s.Amod = s.buf([3, c.KC, 2]); s.Cmod = s.buf([3, c.KC, 2])
        s.esink = s.buf([c.H]); s.negA = s.buf([2 * c.H])
        s.kcol = s.buf([4])
        s.base = s.aoff
        S = s.S
        s.dma("sp", "ld", s.cf.rearrange("p a b -> p (a b)"), s.d_cf, [], ["cf"])
        s.dma("sp", "ld", s.maskA.rearrange("p a b -> p (a b)"), s.d_maskA, [], ["maskA"])
        s.dma("sp", "ld", s.cvec.rearrange("p a b -> p (a b)"), s.d_cvec, [], ["cvec"])
        s.memset("dve", s.kcol[:, 0:1], EPS, ["kcol"])
        s.memset("dve", s.kcol[:, 1:2], 1.0, ["kcol"])
        s.memset("dve", s.onesb, 1.0, ["onesb"])
        s.act(s.sc, s.cvec, AF.Silu, ["cvec"], ["sc"])
        rows = 128
        RB = min(512, c.D)
        for r0 in range(0, c.D, RB):
            s.dma("sp", "ld", s.d_hT[r0:r0 + RB, :], s.d_xT[r0:r0 + RB, :], [], ["hT"])
        stop = s.dbg.get("stop") if s.dbg else None
        for l in range(c.L):
            s.params(l)
            need_ctx = (l < c.L - 1) or c.PERLAYER
            for t in c.tiles:
                s.ffn(l, 0, t)
            if stop == ("ffn0", l): break
            for t in c.tiles:
                s.inproj(l, t)
            if stop == ("inproj", l): break
            s.attn_c(l, need_ctx)
            if stop == ("attnc", l): break
            s.attn_a(l, need_ctx)
            if stop == ("attna", l): break
            s.gdn(l, need_ctx)
            if stop == ("gdn", l): break
            for t in c.tiles:
                if t[2] and not need_ctx:
                    continue
                s.merge(l, t)
            if stop == ("merge", l): break
            for t in c.tiles:
                if t[2] and not need_ctx:
                    continue
                s.ffn(l, 1, t)
        for r0 in range(0, c.D, RB):
            s.dma("sp", "out", s.d_out[r0:r0 + RB, :], s.d_hT[r0:r0 + RB, (0 if c.PERLAYER else c.CTX):c.NT], ["hT"], ["outT"])
        S.emit("sp", "sp", None, ["outT"] + list(S.lastw.keys()), [], False)
        S.replay(s.nc)

    def params(s, l):
        c = s.c
        s.reset()
        KC, RC = c.KC, c.RC
        s.dma("sp", "ld", s.vec, s.d_vecs[l], [], ["vec"])
        s.dma("sp", "ld", s.rowsb, s.d_rows[l], [], ["rows"])
        adown = s.buf([KC, c.RANK])
        s.dma("sp", "ld", adown, s.d_adown[l].rearrange("(kc p) r -> p kc r", p=128), [], ["adown"])
        rT = s.buf([RC, 2])
        psr = Rot(range(8))
        for rc in range(RC):
            b = psr(); pk = "ps%d" % b
            for kc in range(KC):
                s.mm(s.ps[b][:, 0:2], adown[:, kc, rc * 128:(rc + 1) * 128], s.sc[:, kc, :], kc == 0, kc == KC - 1,
                     ["adown", "sc"], [pk])
            s.copy("dve", rT[:, rc, :], s.ps[b][:, 0:2], [pk], ["rT"])
        aups = [s.buf([RC, c.D]) for _ in range(2)]
        for mi in range(9):
            aup = aups[mi % 2]; ak = "aup%d" % (mi % 2)
            s.dma("sp", "ld", aup, s.d_aup[l][:, mi * c.D:(mi + 1) * c.D].rearrange("(rc p) n -> p rc n", p=128), [], [ak])
            b = psr(); pk = "ps%d" % b
            for kc in range(KC):
                for rc in range(RC):
                    s.mm(s.ps[b][:, kc * 2:kc * 2 + 2], aup[:, rc, kc * 128:(kc + 1) * 128], rT[:, rc, :],
                         rc == 0, rc == RC - 1, [ak, "rT"], [pk])
            ab = s.vec[:, c.v_abias + mi * KC: c.v_abias + (mi + 1) * KC].unsqueeze(2).to_broadcast([128, KC, 2])
            s.tt("dve", s.mod[:, mi], s.ps[b][:, 0:KC * 2].rearrange("p (k t) -> p k t", t=2), ab, ALU.add,
                 [pk, "vec"], ["mod"])
        for i in range(3):
            npre = s.vec[:, c.v_npre + i * KC: c.v_npre + (i + 1) * KC].unsqueeze(2).to_broadcast([128, KC, 2])
            npost = s.vec[:, c.v_npost + i * KC: c.v_npost + (i + 1) * KC].unsqueeze(2).to_broadcast([128, KC, 2])
            s.stt(s.Amod[:, i], s.mod[:, 3 * i + 1], 1.0, npre, ALU.add, ALU.mult, ["mod", "vec"], ["Amod"])
            coef = 1.0 if i == 1 else 0.5
            s.stt(s.Cmod[:, i], s.mod[:, 3 * i + 2], coef, npost, ALU.mult, ALU.mult, ["mod", "vec"], ["Cmod"])
        s.act(s.esink, s.rowsb[:, 0:c.H], AF.Exp, ["rows"], ["esink"])
        s.act(s.negA, s.rowsb[:, c.H:3 * c.H], AF.Exp, ["rows"], ["negA0"])
        s.ts("dve", s.negA, s.negA, -1.0, ALU.mult, ["negA0"], ["negA"])
        s.dump("sc", s.sc, ["sc"]); s.dump("rT", rT, ["rT"]); s.dump("mod", s.mod, ["mod"])
        s.dump("Amod", s.Amod, ["Amod"]); s.dump("Cmod", s.Cmod, ["Cmod"]); s.dump("vec", s.vec, ["vec"])
        s.reset()

    def colgroups(s, n):
        return [(g0, min(512, n - g0)) for g0 in range(0, n, 512)]

    def rstd_from_acc(s, acc, acck, n, rstd, rk, psr, scale):
        for (g0, gw) in s.colgroups(n):
            b = psr(); pk = "ps%d" % b
            s.mm(s.ps[b][:, 0:gw], s.ones, acc[:, g0:g0 + gw], True, True, [acck, "cf"], [pk])
            s.act(rstd[:, g0:g0 + gw], s.ps[b][:, 0:gw], AF.Sqrt, [pk, "kcol"], [rk + "s"], bias=s.kcol[:, 0:1], scale=scale)
        s.recip(rstd[:, 0:n], rstd[:, 0:n], [rk + "s"], [rk])

    def norm_u(s, l, sub, tile, uT, uk, psr):
        c = s.c
        c0, n, isctx = tile
        cls = 1 if isctx else 0
        KC = c.KC
        m0 = s.aoff
        acc = s.buf([c.TT]); rstd = s.buf([c.TT])
        hb = [s.buf([c.TT]) for _ in range(2)]
        sq = [s.buf([c.TT]) for _ in range(2)]
        ak = s.key("acc"); rk = s.key("rstd")
        s.memset("pool", acc[:, 0:n], 0.0, [ak])
        for kc in range(KC):
            h = hb[kc % 2]; hk = "hb%d" % (kc % 2)
            s.dma("sp", "ld", h[:, 0:n], s.d_hT[kc * 128:(kc + 1) * 128, c0:c0 + n], ["hT"], [hk])
            q = sq[kc % 2]; qk = "sq%d" % (kc % 2)
            s.act(q[:, 0:n], h[:, 0:n], AF.Square, [hk], [qk])
            s.tt("dve", acc[:, 0:n], acc[:, 0:n], q[:, 0:n], ALU.add, [ak, qk], [ak])
        s.rstd_from_acc(acc, ak, n, rstd, rk, psr, 1.0 / c.D)
        for kc in range(KC):
            h = hb[kc % 2]; hk = "hb%d" % (kc % 2)
            s.dma("sp", "ld", h[:, 0:n], s.d_hT[kc * 128:(kc + 1) * 128, c0:c0 + n], ["hT"], [hk])
            q = sq[kc % 2]; qk = "sq%d" % (kc % 2)
            s.stt(q[:, 0:n], h[:, 0:n], s.Amod[:, sub, kc, cls:cls + 1], rstd[:, 0:n], ALU.mult, ALU.mult,
                  [hk, rk, "Amod"], [qk])
            s.act(uT[:, kc, 0:n], q[:, 0:n], AF.Identity, [qk, "mod"], [uk + str(kc)],
                  bias=s.mod[:, 3 * sub, kc, cls:cls + 1], scale=1.0)
        if not isctx:
            s.dump("rstd", rstd, [rk]); s.dump("uT", uT, [uk + str(k) for k in range(KC)], BF16)
        s.reset(m0)

    def wload(s, wbufs, wi, src, kcn, w):
        slot = wi % len(wbufs)
        wb = wbufs[slot]; wk = "wb%d" % slot
        s.dma("pool", "w%d" % slot, wb[:, 0:kcn, 0:w], src.rearrange("(kc p) n -> p kc n", p=128), [], [wk, wk + "u"])
        return wb, wk

    def proj_fm(s, uT, uk, kcn, n, wsrc, chunks, wbufs, psr, consume, wi0=0):
        gsz = wbufs[0].shape[2] // 128
        wi = wi0
        for i0 in range(0, len(chunks), gsz):
            grp = chunks[i0:i0 + gsz]
            col0 = grp[0][0]; wtot = sum(w for _, w in grp)
            wb, wk = s.wload(wbufs, wi, wsrc(col0, wtot), kcn, wtot); wi += 1
            off = 0
            for ci, (cc, w) in enumerate(grp):
                for (g0, gw) in s.colgroups(n):
                    b = psr(); pk = "ps%d" % b
                    for kc in range(kcn):
                        s.mm(s.ps[b][0:w, 0:gw], wb[:, kc, off:off + w], uT[:, kc, g0:g0 + gw], kc == 0, kc == kcn - 1,
                             [wk, uk + str(kc)], [pk])
                    consume(i0 + ci, b, g0, gw)
                off += w
        return wi

    def proj_tok(s, uT, uk, kcn, n, wsrc, col0, w, wbufs, wi, psr, consume):
        wb, wk = s.wload(wbufs, wi, wsrc(col0, w), kcn, w)
        for tb in range(n // 128):
            b = psr(); pk = "ps%d" % b
            for kc in range(kcn):
                s.mm(s.ps[b][:, 0:w], uT[:, kc, tb * 128:(tb + 1) * 128], wb[:, kc, 0:w], kc == 0, kc == kcn - 1,
                     [wk, uk + str(kc)], [pk])
            consume(tb, b)
        return wi + 1

    def post(s, l, sub, tile, acc2, a2k, psr):
        c = s.c
        c0, n, isctx = tile
        cls = 1 if isctx else 0
        m0 = s.aoff
        rstd = s.buf([c.TT]); rk = s.key("rstd2")
        s.rstd_from_acc(acc2, a2k, n, rstd, rk, psr, 1.0 / c.D)
        yb = [s.buf([c.TT]) for _ in range(2)]
        hb = [s.buf([c.TT]) for _ in range(2)]
        ob = [s.buf([c.TT]) for _ in range(2)]
        for kc in range(c.KC):
            i = kc % 2
            s.dma("sp", "ld", yb[i][:, 0:n], s.d_yT[kc * 128:(kc + 1) * 128, c0:c0 + n], ["yT"], ["pyb%d" % i])
            s.dma("sp", "ld", hb[i][:, 0:n], s.d_hT[kc * 128:(kc + 1) * 128, c0:c0 + n], ["hT"], ["phb%d" % i])
            s.tt("dve", yb[i][:, 0:n], yb[i][:, 0:n], rstd[:, 0:n], ALU.mult, ["pyb%d" % i, rk], ["pyb%d" % i])
            s.stt(ob[i][:, 0:n], yb[i][:, 0:n], s.Cmod[:, sub, kc, cls:cls + 1], hb[i][:, 0:n], ALU.mult, ALU.add,
                  ["pyb%d" % i, "phb%d" % i, "Cmod"], ["pob%d" % i])
            s.dma("act", "st", s.d_hT[kc * 128:(kc + 1) * 128, c0:c0 + n], ob[i][:, 0:n], ["pob%d" % i], ["hT"])
        s.reset(m0)

    def y_consume(s, c0, n, acc2, a2k, ybufs):
        c = s.c
        st = {"i": 0}

        def consume(ci, b, g0, gw):
            pk = "ps%d" % b
            slot = (ci % 2)
            yb = ybufs[slot]; yk = "yb%d" % slot
            sqb = ybufs[2 + slot]; sk = "ysq%d" % slot
            s.act(yb[:, g0:g0 + gw], s.ps[b][:, 0:gw], AF.Copy, [pk], [yk])
            s.act(sqb[:, g0:g0 + gw], s.ps[b][:, 0:gw], AF.Square, [pk], [sk])
            s.tt("dve", acc2[:, g0:g0 + gw], acc2[:, g0:g0 + gw], sqb[:, g0:g0 + gw], ALU.add, [a2k, sk], [a2k])
            if g0 + gw >= n:
                s.dma("sp", "st", s.d_yT[ci * 128:(ci + 1) * 128, c0:c0 + n], yb[:, 0:n], [yk], ["yT"])
        return consume

    def ffn(s, l, which, tile):
        c = s.c
        c0, n, isctx = tile
        sub = 0 if which == 0 else 2
        s.reset()
        psr = Rot(range(8))
        acc2 = s.buf([c.TT]); a2k = s.key("acc2")
        mpost = s.aoff
        uT = s.buf([c.KC, c.TT], BF16); uk = s.key("uT")
        aT = s.buf([c.FC, c.TT], BF16); ak = s.key("aT")
        wbufs = [s.buf([max(c.KC, c.FC), 256], BF16) for _ in range(3)]
        s.norm_u(l, sub, tile, uT, uk, psr)
        m0 = s.aoff
        sgb = [s.buf([512]) for _ in range(2)]
        wgu = s.d_wgu[l, which]
        wi = 0
        for j in range(c.FC):
            slot = wi % 3; wb = wbufs[slot]; wk = "wb%d" % slot; wi += 1
            s.dma("pool", "w%d" % slot, wb[:, 0:c.KC, 0:128],
                  wgu[:, j * 128:(j + 1) * 128].rearrange("(kc p) n -> p kc n", p=128), [], [wk])
            s.dma("pool", "w%d" % slot, wb[:, 0:c.KC, 128:256],
                  wgu[:, c.DFF + j * 128: c.DFF + (j + 1) * 128].rearrange("(kc p) n -> p kc n", p=128), [], [wk + "u"])
            for gi, (g0, gw) in enumerate(s.colgroups(n)):
                bg = psr(); bu = psr()
                for kc in range(c.KC):
                    s.mm(s.ps[bg][:, 0:gw], wb[:, kc, 0:128], uT[:, kc, g0:g0 + gw], kc == 0, kc == c.KC - 1,
                         [wk, uk + str(kc)], ["ps%d" % bg])
                for kc in range(c.KC):
                    s.mm(s.ps[bu][:, 0:gw], wb[:, kc, 128:256], uT[:, kc, g0:g0 + gw], kc == 0, kc == c.KC - 1,
                         [wk + "u", uk + str(kc)], ["ps%d" % bu])
                sg = sgb[gi % 2]; sk = "sg%d" % (gi % 2)
                s.act(sg[:, 0:gw], s.ps[bg][:, 0:gw], AF.Silu, ["ps%d" % bg], [sk])
                s.tt("dve", aT[:, j, g0:g0 + gw], sg[:, 0:gw], s.ps[bu][:, 0:gw], ALU.mult, [sk, "ps%d" % bu],
                     [ak + str(j)])
        s.reset(m0)
        ybufs = [s.buf([c.TT]) for _ in range(4)]
        s.memset("pool", acc2[:, 0:n], 0.0, [a2k])
        wd = s.d_wd[l, which]
        s.proj_fm(aT, ak, c.FC, n, lambda col0, w: wd[:, col0:col0 + w], [(i * 128, 128) for i in range(c.KC)],
                  wbufs, psr, s.y_consume(c0, n, acc2, a2k, ybufs), wi0=wi)
        s.reset(mpost)
        s.post(l, sub, tile, acc2, a2k, psr)
        s.reset()

    def inproj(s, l, tile):
        c = s.c
        c0, n, isctx = tile
        H, KV = c.H, c.KV
        s.reset()
        psr = Rot(range(8))
        uT = s.buf([c.KC, c.TT], BF16); uk = s.key("uT")
        wbufs = [s.buf([c.KC, 256], BF16) for _ in range(3)]
        s.norm_u(l, 1, tile, uT, uk, psr)
        win = s.d_win[l]
        wsrc = lambda col0, w: win[:, col0:col0 + w]
        TT = c.TT
        xb = [s.buf([TT]) for _ in range(2)]
        t1 = [s.buf([TT]) for _ in range(2)]
        t2 = [s.buf([TT]) for _ in range(2)]
        ob = [s.buf([TT], BF16) for _ in range(3)]
        of = [s.buf([TT]) for _ in range(2)]
        rb = [s.buf([TT]) for _ in range(2)]
        if not isctx:
            cosb = s.buf([TT]); sinb = s.buf([TT])
            l0 = c0 - c.CTX
            s.dma("sp", "ld", cosb[:, 0:n], s.d_rope[0, :, l0:l0 + n], [], ["cosb"])
            s.dma("sp", "ld", sinb[:, 0:n], s.d_rope[1, :, l0:l0 + n], [], ["sinb"])
        cnt = {"x": 0, "o": 0, "f": 0}

        def rope_store(xa, xk, g0, gw, dst, dkey, oslot):
            o = ob[oslot]; ok = "ob%d" % oslot
            if isctx:
                s.copy("pool", o[:, g0:g0 + gw], xa, [xk], [ok])
            else:
                b = psr(); pk = "ps%d" % b
                s.mm(s.ps[b][:, 0:gw], s.permR, xa, True, True, ["cf", xk], [pk])
                i = cnt["x"] % 2
                s.tt("pool", t1[i][:, g0:g0 + gw], xa, cosb[:, g0:g0 + gw], ALU.mult, [xk, "cosb"], ["t1%d" % i])
                s.tt("dve", t2[i][:, g0:g0 + gw], s.ps[b][:, 0:gw], sinb[:, g0:g0 + gw], ALU.mult, [pk, "sinb"], ["t2%d" % i])
                s.tt("pool", o[:, g0:g0 + gw], t1[i][:, g0:g0 + gw], t2[i][:, g0:g0 + gw], ALU.add,
                     ["t1%d" % i, "t2%d" % i], [ok])
            if g0 + gw >= n:
                s.dma("sp", "st", dst, o[:, 0:n], [ok], [dkey])

        def mk_qk(dstT, dkey, nrm):
            def consume(ci, b, g0, gw):
                pk = "ps%d" % b
                i = cnt["x"] % 2; cnt["x"] += 1
                x = xb[i]; xk = "xb%d" % i
                if g0 == 0:
                    cnt["o"] += 1
                oslot = cnt["o"] % 3
                s.act(x[:, g0:g0 + gw], s.ps[b][:, 0:gw], AF.Copy, [pk], [xk])
                if nrm is not None:
                    r = rb[i]; rk = "rb%d" % i
                    s.act(t1[i][:, g0:g0 + gw], s.ps[b][:, 0:gw], AF.Square, [pk], ["t1%d" % i])
                    b2 = psr(); pk2 = "ps%d" % b2
                    s.mm(s.ps[b2][:, 0:gw], s.ones, t1[i][:, g0:g0 + gw], True, True, ["cf", "t1%d" % i], [pk2])
                    s.act(r[:, g0:g0 + gw], s.ps[b2][:, 0:gw], AF.Sqrt, [pk2, "kcol"], [rk + "s"], bias=s.kcol[:, 0:1],
                          scale=1.0 / 128)
                    s.recip(r[:, g0:g0 + gw], r[:, g0:g0 + gw], [rk + "s"], [rk])
                    s.stt(x[:, g0:g0 + gw], x[:, g0:g0 + gw], s.vec[:, nrm:nrm + 1], r[:, g0:g0 + gw], ALU.mult, ALU.mult,
                          [xk, rk, "vec"], [xk])
                rope_store(x[:, g0:g0 + gw], xk, g0, gw, dstT[ci][:, c0:c0 + n], dkey, oslot)
            return consume

        def mk_plain(dst_fn, dkey, func, dt):
            def consume(ci, b, g0, gw):
                pk = "ps%d" % b
                if dt == BF16:
                    if g0 == 0:
                        cnt["o"] += 1
                    i = cnt["o"] % 3; o = ob[i]; ok = "ob%d" % i
                else:
                    if g0 == 0:
                        cnt["f"] += 1
                    i = cnt["f"] % 2; o = of[i]; ok = "of%d" % i
                s.act(o[:, g0:g0 + gw], s.ps[b][:, 0:gw], func, [pk], [ok])
                if g0 + gw >= n:
                    s.dma("sp", "st", dst_fn(ci), o[:, 0:n], [ok], [dkey])
            return consume

        def mk_tok(dst, dkey, w, dt):
            def consume(tb, b):
                pk = "ps%d" % b
                if dt == BF16:
                    cnt["o"] += 1
                    i = cnt["o"] % 3; o = ob[i]; ok = "ob%d" % i
                else:
                    cnt["f"] += 1
                    i = cnt["f"] % 2; o = of[i]; ok = "of%d" % i
                s.act(o[:, 0:w], s.ps[b][:, 0:w], AF.Copy, [pk], [ok])
                s.dma("sp", "st", dst[c0 + tb * 128: c0 + (tb + 1) * 128, :], o[:, 0:w], [ok], [dkey])
            return consume

        ch = lambda o0, k: [(o0 + i * 128, 128) for i in range(k)]
        wi = 0
        wi = s.proj_fm(uT, uk, c.KC, n, wsrc, ch(c.o_aq, H), wbufs, psr, mk_qk(s.d_qa, "qa", None), wi)
        wi = s.proj_fm(uT, uk, c.KC, n, wsrc, ch(c.o_ak, KV), wbufs, psr, mk_qk(s.d_ka, "ka", None), wi)
        wi = s.proj_tok(uT, uk, c.KC, n, wsrc, c.o_av, KV * 128, wbufs, wi, psr, mk_tok(s.d_va, "va", KV * 128, BF16))
        wi = s.proj_fm(uT, uk, c.KC, n, wsrc, ch(c.o_bqkv, 3 * H), wbufs, psr,
                       mk_plain(lambda ci: s.d_braw[ci][:, c0:c0 + n], "braw", AF.Copy, F32), wi)
        wi = s.proj_fm(uT, uk, c.KC, n, wsrc, ch(c.o_bz, H), wbufs, psr,
                       mk_plain(lambda ci: s.d_zs[ci][:, c0:c0 + n], "zs", AF.Silu, BF16), wi)
        wi = s.proj_tok(uT, uk, c.KC, n, wsrc, c.o_bg, 4 * H, wbufs, wi, psr, mk_tok(s.d_gtok, "gtok", 4 * H, F32))
        wi = s.proj_fm(uT, uk, c.KC, n, wsrc, ch(c.o_cq, H), wbufs, psr, mk_qk(s.d_qc, "qc", c.v_cq), wi)
        wi = s.proj_fm(uT, uk, c.KC, n, wsrc, ch(c.o_ck, KV), wbufs, psr, mk_qk(s.d_kc, "kc", c.v_ck), wi)
        wi = s.proj_tok(uT, uk, c.KC, n, wsrc, c.o_cv, KV * 128, wbufs, wi, psr, mk_tok(s.d_vc, "vc", KV * 128, BF16))
        wi = s.proj_fm(uT, uk, c.KC, n, wsrc, ch(c.o_mg, 3 * c.KC), wbufs, psr,
                       mk_plain(lambda ci: s.d_mg[ci // c.KC][(ci % c.KC) * 128:(ci % c.KC + 1) * 128, c0:c0 + n],
                                "mg", AF.Sigmoid, BF16), wi)
        s.reset()

    def attn_c(s, l, need_ctx):
        attn_c(s, l, need_ctx)

    def attn_a(s, l, need_ctx):
        attn_a(s, l, need_ctx)

    def gdn(s, l, need_ctx):
        gdn(s, l, need_ctx)

    def merge(s, l, tile):
        merge(s, l, tile)


def attn_c(s, l, need_ctx):
    c = s.c
    s.reset()
    NT, NCK = c.NT, c.NT // 128
    scale = 1.0 / math.sqrt(128.0)
    kT = s.buf([NT], BF16); vt = s.buf([NCK, 128], BF16)
    qb = [s.buf([512], BF16) for _ in range(2)]
    pb = [s.buf([512], BF16) for _ in range(3)]
    rd = [s.buf([512]) for _ in range(2)]
    ob = [s.buf([512], BF16) for _ in range(2)]
    rS = Rot([0, 1, 2]); rO = Rot([3, 4]); rD = Rot([5, 6])
    it = 0
    for g in range(c.KV):
        s.dma("sp", "ld", kT, s.d_kc[g], ["kc"], ["ckT"])
        s.dma("sp", "ld", vt, s.d_vc[:, g * 128:(g + 1) * 128].rearrange("(ck p) d -> p ck d", p=128), ["vc"], ["cvt"])
        for hh in range(c.G):
            h = g * c.G + hh
            qgroups = []
            if need_ctx:
                qgroups += [(q0, min(512, c.CTX - q0), c.CTX // 128) for q0 in range(0, c.CTX, 512)]
            qgroups += [(c.CTX + q0, min(512, c.SEQ - q0), NCK) for q0 in range(0, c.SEQ, 512)]
            for (q0, qw, nk) in qgroups:
                i = it % 2; it += 1
                q = qb[i]; qk = "cq%d" % i
                s.dma("sp", "ld", q[:, 0:qw], s.d_qc[h][:, q0:q0 + qw], ["qc"], [qk])
                bo = rO(); bd = rD()
                for ck in range(nk):
                    bs = rS(); pk = "ps%d" % bs
                    s.mm(s.ps[bs][:, 0:qw], kT[:, ck * 128:(ck + 1) * 128], q[:, 0:qw], True, True, ["ckT", qk], [pk])
                    pi = (ck % 3); p = pb[pi]; ppk = "cp%d" % pi
                    s.act(p[:, 0:qw], s.ps[bs][:, 0:qw], AF.Exp, [pk], [ppk], scale=scale)
                    s.mm(s.ps[bo][:, 0:qw], vt[:, ck, :], p[:, 0:qw], ck == 0, ck == nk - 1, ["cvt", ppk], ["ps%d" % bo])
                    s.mm(s.ps[bd][:, 0:qw], s.onesb, p[:, 0:qw], ck == 0, ck == nk - 1, ["onesb", ppk], ["ps%d" % bd])
                s.recip(rd[i][:, 0:qw], s.ps[bd][:, 0:qw], ["ps%d" % bd], ["crd%d" % i])
                s.tt("dve", ob[i][:, 0:qw], s.ps[bo][:, 0:qw], rd[i][:, 0:qw], ALU.mult, ["ps%d" % bo, "crd%d" % i], ["cob%d" % i])
                s.dma("pool", "st2", s.d_oT[2, h][:, q0:q0 + qw], ob[i][:, 0:qw], ["cob%d" % i], ["oT"])
    s.reset()


def attn_a(s, l, need_ctx):
    c = s.c
    s.reset()
    NT, NCK, G = c.NT, c.NT // 128, c.G
    GW = G * 128
    scale = 1.0 / math.sqrt(128.0)
    kT = s.buf([NT], BF16); vt = s.buf([NCK, 128], BF16)
    qa = s.buf([G, NT], BF16)
    pb = [s.buf([GW], BF16) for _ in range(3)]
    den = [s.buf([GW]) for _ in range(2)]
    ob = [s.buf([G, 128], BF16) for _ in range(2)]
    rS = Rot([0, 1, 2]); rO = Rot([3, 4]); rD = Rot([5, 6])
    nctx = c.CTX // 128
    nb = c.SEQ // 128
    it = 0
    for g in range(c.KV):
        s.dma("sp", "ld", kT, s.d_ka[g], ["ka"], ["akT"])
        s.dma("sp", "ld", vt, s.d_va[:, g * 128:(g + 1) * 128].rearrange("(ck p) d -> p ck d", p=128), ["va"], ["avt"])
        s.dma("sp", "ld", qa, s.d_qa[g * G:(g + 1) * G].rearrange("h d t -> d h t"), ["qa"], ["aqa"])
        blocks = []
        if need_ctx:
            for n in range(nctx):
                blocks.append((n, [(k, None) for k in range(nctx)]))
        for n in range(nb):
            ks = [(k, None) for k in range(nctx)]
            if n > 0:
                ks.append((nctx + n - 1, 0))
            ks.append((nctx + n, None))
            if n < nb - 1:
                ks.append((nctx + n + 1, 1))
            blocks.append((nctx + n, ks))
        for (qblk, ks) in blocks:
            i = it % 2; it += 1
            q = qa[:, :, qblk * 128:(qblk + 1) * 128]
            bo = rO(); bd = rD()
            for j, (ck, mk) in enumerate(ks):
                bs = rS(); pk = "ps%d" % bs
                s.mm(s.ps[bs][:, 0:GW].rearrange("p (g t) -> p g t", g=G), kT[:, ck * 128:(ck + 1) * 128], q, True, True,
                     ["akT", "aqa"], [pk])
                pi = j % 3; p = pb[pi]; ppk = "ap%d" % pi
                s.act(p, s.ps[bs][:, 0:GW], AF.Exp, [pk], [ppk], scale=scale)
                if mk is not None:
                    s.tt("pool", p, p, s.maskA[:, mk, :], ALU.mult, [ppk, "maskA"], [ppk])
                s.mm(s.ps[bo][:, 0:GW], vt[:, ck, :], p, j == 0, j == len(ks) - 1, ["avt", ppk], ["ps%d" % bo])
                s.mm(s.ps[bd][:, 0:GW], s.onesb, p, j == 0, j == len(ks) - 1, ["onesb", ppk], ["ps%d" % bd])
            d = den[i]; dk = "aden%d" % i
            for hh in range(G):
                h = g * G + hh
                s.ts("dve", d[:, hh * 128:(hh + 1) * 128], s.ps[bd][:, hh * 128:(hh + 1) * 128], s.esink[:, h:h + 1], ALU.add,
                     ["ps%d" % bd, "esink"], [dk])
            s.recip(d, d, [dk], [dk])
            s.tt("dve", ob[i].rearrange("p g t -> p (g t)"), s.ps[bo][:, 0:GW], d, ALU.mult, ["ps%d" % bo, dk], ["aob%d" % i])
            s.dma("pool", "st2", s.d_oT[0, g * G:(g + 1) * G, :, qblk * 128:(qblk + 1) * 128].rearrange("h d t -> d h t"),
                  ob[i], ["aob%d" % i], ["oT"])
    s.reset()


def merge(s, l, tile):
    c = s.c
    c0, n, isctx = tile
    H, KC = c.H, c.KC
    s.reset()
    psr = Rot(range(8))
    acc2 = s.buf([c.TT]); a2k = s.key("acc2")
    mpost = s.aoff
    mT = s.buf([KC, c.TT], BF16); mk = s.key("mT")
    wbufs = [s.buf([KC, 256], BF16) for _ in range(2)]
    m1 = s.aoff
    oT = s.buf([3 * H, c.TT], BF16)
    wbr = [s.buf([3 * H, 128], BF16) for _ in range(2)]
    gb = [s.buf([3, c.TT], BF16) for _ in range(2)]
    tm = [s.buf([512]) for _ in range(4)]
    for i in range(3):
        s.dma("sp", "ld", oT[:, i * H:(i + 1) * H, 0:n],
              s.d_oT[i][:, :, c0:c0 + n].rearrange("h d t -> d h t"), ["oT"], ["moT%d" % i])
    it = 0
    for fc in range(KC):
        wslot = fc % 2; wb = wbr[wslot]; wk = "wbr%d" % wslot
        for i in range(3):
            s.dma("pool", "wr%d" % wslot, wb[:, i * H:(i + 1) * H, :],
                  s.d_wbr[l, i][:, fc * 128:(fc + 1) * 128].rearrange("(kc p) n -> p kc n", p=128), [], [wk + str(i)])
        g = gb[wslot]; gk = "mgb%d" % wslot
        s.dma("sp", "ld", g[:, :, 0:n], s.d_mg[:, fc * 128:(fc + 1) * 128, c0:c0 + n].rearrange("i p t -> p i t"), ["mg"], [gk])
        for (g0, gw) in s.colgroups(n):
            bs = []
            for i in range(3):
                b = psr(); bs.append(b)
                for k in range(H):
                    s.mm(s.ps[b][:, 0:gw], wb[:, i * H + k, :], oT[:, i * H + k, g0:g0 + gw], k == 0, k == H - 1,
                         [wk + str(i), "moT%d" % i], ["ps%d" % b])
            j = it % 2; it += 1
            ta = tm[2 * j]; tb_ = tm[2 * j + 1]; tak = "mta%d" % j; tbk = "mtb%d" % j
            s.tt("dve", ta[:, 0:gw], s.ps[bs[0]][:, 0:gw], g[:, 0, g0:g0 + gw], ALU.mult, ["ps%d" % bs[0], gk], [tak])
            s.tt("dve", tb_[:, 0:gw], s.ps[bs[1]][:, 0:gw], g[:, 1, g0:g0 + gw], ALU.mult, ["ps%d" % bs[1], gk], [tbk])
            s.tt("pool", ta[:, 0:gw], ta[:, 0:gw], tb_[:, 0:gw], ALU.add, [tak, tbk], [tak])
            s.tt("dve", tb_[:, 0:gw], s.ps[bs[2]][:, 0:gw], g[:, 2, g0:g0 + gw], ALU.mult, ["ps%d" % bs[2], gk], [tbk])
            s.tt("pool", mT[:, fc, g0:g0 + gw], ta[:, 0:gw], tb_[:, 0:gw], ALU.add, [tak, tbk], [mk + str(fc)])
    s.reset(m1)
    ybufs = [s.buf([c.TT]) for _ in range(4)]
    s.memset("pool", acc2[:, 0:n], 0.0, [a2k])
    wo = s.d_wout[l]
    s.proj_fm(mT, mk, KC, n, lambda col0, w: wo[:, col0:col0 + w], [(i * 128, 128) for i in range(KC)],
              wbufs, psr, s.y_consume(c0, n, acc2, a2k, ybufs))
    s.reset(mpost)
    s.post(l, 1, tile, acc2, a2k, psr)
    s.reset()


def gdn(s, l, need_ctx):
    c = s.c
    H, NT, CTX, SEQ = c.H, c.NT, c.CTX, c.SEQ
    HW = H * 64; HD = H * 128
    s.reset()
    psr = Rot(range(8))
    W = 512
    xin = [s.buf([W + 2]) for _ in range(2)]
    yb = [s.buf([W]) for _ in range(2)]
    sb = [s.buf([W]) for _ in range(2)]
    qb = [s.buf([W]) for _ in range(2)]
    rb = [s.buf([W]) for _ in range(2)]
    tb = [s.buf([H, 128]) for _ in range(2)]
    it = 0
    if s.dbg and s.dbg.get('exp2'):
        dmy = s.buf([8])
        s.memset("dve", dmy, 0.5, ["dmy"])
        s.act(dmy, dmy, AF.Silu, ["dmy"], ["dmy"])
    seqs = [(0, CTX), (CTX, SEQ)]
    for (s0, slen) in seqs:
        for b0 in range(0, slen, W):
            w = min(W, slen - b0)
            t0 = s0 + b0
            nblk = w // 128
            for kind in (0, 1, 2):
                for h in (list(reversed(range(H))) if s.dbg and s.dbg.get('exp1') else range(H)):
                    fc = kind * H + h
                    i = it % 2; it += 1
                    x = xin[i]; xk = "gx%d" % i
                    lo = 1 if b0 == 0 else 0
                    hi = 1 if b0 + w >= slen else 0
                    if lo:
                        s.memset("pool", x[:, 0:1], 0.0, [xk + "l"])
                    if hi:
                        s.memset("pool", x[:, w + 1:w + 2], 0.0, [xk + "h"])
                    s.dma("sp", "ld", x[:, lo:w + 2 - hi], s.d_braw[fc][:, t0 - 1 + lo: t0 + w + 1 - hi], ["braw"], [xk])
                    cw = lambda j: s.vec[:, c.v_bconv + j * 3 * H + fc: c.v_bconv + j * 3 * H + fc + 1]
                    y = yb[i]; yk = "gy%d" % i
                    s.ts("dve", y[:, 0:w], x[:, 0:w], cw(0), ALU.mult, [xk, xk + "l", "vec"], [yk])
                    s.stt(y[:, 0:w], x[:, 1:w + 1], cw(1), y[:, 0:w], ALU.mult, ALU.add, [xk, yk, "vec"], [yk])
                    s.stt(y[:, 0:w], x[:, 2:w + 2], cw(2), y[:, 0:w], ALU.mult, ALU.add, [xk, xk + "h", yk, "vec"], [yk])
                    sv = sb[i]; sk = "gs%d" % i
                    s.dump("g_x", x[:, 0:w + 2], [xk, xk + "l", xk + "h"]); s.dump("g_y", y[:, 0:w], [yk])
                    s.act(sv[:, 0:w], y[:, 0:w], AF.Silu, [yk], [sk])
                    s.dump("g_s", sv[:, 0:w], [sk])
                    if kind < 2:
                        q = qb[i]; qk = "gq%d" % i
                        s.act(q[:, 0:w], sv[:, 0:w], AF.Square, [sk], [qk])
                        b = psr(); pk = "ps%d" % b
                        s.mm(s.ps[b][:, 0:w], s.ones, q[:, 0:w], True, True, ["cf", qk], [pk])
                        r = rb[i]; rk = "gr%d" % i
                        s.act(r[:, 0:w], s.ps[b][:, 0:w], AF.Sqrt, [pk, "kcol"], [rk + "s"], bias=s.kcol[:, 0:1], scale=1.0)
                        s.recip(r[:, 0:w], r[:, 0:w], [rk + "s"], [rk])
                        if kind == 0:
                            s.stt(sv[:, 0:w], sv[:, 0:w], 1.0 / math.sqrt(128.0), r[:, 0:w], ALU.mult, ALU.mult, [sk, rk], [sk])
                            s.dma("act", "st", s.d_gq[h][:, t0:t0 + w], sv[:, 0:w], [sk], ["gq"])
                        else:
                            s.tt("dve", sv[:, 0:w], sv[:, 0:w], r[:, 0:w], ALU.mult, [sk, rk], [sk])
                            s.dma("act", "st", s.d_gk[h][:, t0:t0 + w], sv[:, 0:w], [sk], ["gk"])
                    if kind >= 1:
                        dst = s.d_gkt if kind == 1 else s.d_gvt
                        dk = "gkt" if kind == 1 else "gvt"
                        for bl in range(nblk):
                            b = psr(); pk = "ps%d" % b
                            s.S.emit("pe", "pe", (lambda e, o=s.ps[b][:, 0:128], a=sv[:, bl * 128:(bl + 1) * 128]:
                                                  e.transpose(out=o, in_=a, identity=s.ident)), [sk, "cf"], [pk], False)
                            j = (it + bl) % 2
                            tt_ = tb[j]; tk = "gt%d" % j
                            s.copy("dve", tt_[:, 0, :], s.ps[b][:, 0:128], [pk], [tk])
                            s.dma("act", "st", dst[t0 + bl * 128: t0 + (bl + 1) * 128, h * 128:(h + 1) * 128], tt_[:, 0, :], [tk], [dk])
    if s.dbg and s.dbg.get("gstop") == "prep":
        s.reset(); return
    s.reset()
    s.g64 = s.buf([2 * 64 + 5 * c.H * 64])
    s.dma("sp", "ld", s.g64[0:64, :], s.d_g64, [], ["g64"])
    G0 = 0; TRI = [s.g64[0:64, 0:64], s.g64[0:64, 64:128]]
    o0 = 128
    IREP = s.g64[0:64, o0:o0 + HW]
    NEG = [s.g64[0:64, o0 + HW:o0 + 2 * HW], s.g64[0:64, o0 + 2 * HW:o0 + 3 * HW]]
    STR = [s.g64[0:64, o0 + 3 * HW:o0 + 4 * HW], s.g64[0:64, o0 + 4 * HW:o0 + 5 * HW]]
    I64 = s.ident[0:64, 0:64]
    ONES64 = s.ones[0:64, :]
    one64 = s.kcol[0:64, 1:2]
    rS1 = Rot([4, 5, 6, 7])

    def v3(ap, a):
        return ap.rearrange("p (a b) -> p a b", a=a)

    def bc(ap, n):
        return ap.unsqueeze(2).to_broadcast([ap.shape[0], H, n])

    D = []
    for d in range(2):
        B_ = {}
        for nm, sz in (("qT", HW), ("kT", HW), ("k", HD), ("v", HD), ("gt", 4 * H), ("x0", H), ("g", H), ("beta", H),
                       ("gc", H), ("egc", H), ("ekd", H), ("bk", H), ("negb", H), ("eglast", H), ("Dg", HW), ("erow", HW),
                       ("dm", HW), ("decs", HW), ("A0", HW), ("A1", HW), ("B0", HW), ("B1", HW), ("QK", HW), ("QKT", HW),
                       ("T0", HW), ("T1", HW), ("vb", HD), ("kbe", HD), ("kd", HD), ("u", HD), ("wT", HW), ("qdT", HW),
                       ("vnew", HD), ("o", HD), ("S", HD)):
            B_[nm] = s.buf([sz])
        D.append(B_)
        s.memset("pool", B_["S"], 0.0, ["S%d" % d])

    def K_(d, nm):
        return "g%d%s" % (d, nm)

    def stage1(d, t0):
        B_ = D[d]
        k = lambda nm: K_(d, nm)
        q64 = lambda nm: B_[nm][0:64]
        s.dma("sp", "ld", v3(B_["qT"], H), s.d_gq[:, :, t0:t0 + 64].rearrange("h d t -> d h t"), ["gq"], [k("qT")])
        s.dma("sp", "ld", v3(B_["kT"], H), s.d_gk[:, :, t0:t0 + 64].rearrange("h d t -> d h t"), ["gk"], [k("kT")])
        s.dma("sp", "ld", q64("k"), s.d_gkt[t0:t0 + 64, :], ["gkt"], [k("k")])
        s.dma("sp", "ld", q64("v"), s.d_gvt[t0:t0 + 64, :], ["gvt"], [k("v")])
        s.dma("sp", "ld", q64("gt"), s.d_gtok[t0:t0 + 64, :], ["gtok"], [k("gt")])
        gt = q64("gt")
        s.tt("dve", q64("x0"), gt[:, d * 2 * H: d * 2 * H + H], s.rowsb[0:64, 3 * H + d * H: 3 * H + (d + 1) * H], ALU.add,
             [k("gt"), "rows"], [k("x0")])
        s.act(q64("x0"), q64("x0"), AF.Exp, [k("x0")], [k("x0")])
        s.act(q64("x0"), q64("x0"), AF.Ln, [k("x0"), "kcol"], [k("x0")], bias=one64, scale=1.0)
        s.tt("dve", q64("g"), q64("x0"), s.negA[0:64, d * H:(d + 1) * H], ALU.mult, [k("x0"), "negA"], [k("g")])
        s.act(q64("beta"), gt[:, d * 2 * H + H: (d + 1) * 2 * H], AF.Exp, [k("gt")], [k("beta")], scale=-1.0)
        s.ts("dve", q64("beta"), q64("beta"), 1.0, ALU.add, [k("beta")], [k("beta")])
        s.recip(q64("beta"), q64("beta"), [k("beta")], [k("beta")])
        b = rS1(); pk = "ps%d" % b
        s.mm(s.ps[b][0:64, 0:H], TRI[d], q64("g"), True, True, ["g64", k("g")], [pk])
        s.copy("dve", q64("gc"), s.ps[b][0:64, 0:H], [pk], [k("gc")])
        b2 = rS1(); pk2 = "ps%d" % b2
        s.mm(s.ps[b2][:, 0:H], ONES64, q64("g"), True, True, ["cf", k("g")], [pk2])
        s.act(B_["eglast"], s.ps[b2][:, 0:H], AF.Exp, [pk2], [k("eglast")])
        s.tt("dve", q64("ekd"), s.ps[b2][0:64, 0:H], q64("gc"), ALU.subtract, [pk2, k("gc")], [k("ekd")])
        s.act(q64("ekd"), q64("ekd"), AF.Exp, [k("ekd")], [k("ekd")])
        s.act(q64("egc"), q64("gc"), AF.Exp, [k("gc")], [k("egc")])
        s.tt("dve", q64("bk"), q64("beta"), q64("egc"), ALU.mult, [k("beta"), k("egc")], [k("bk")])
        s.ts("dve", q64("negb"), q64("beta"), -1.0, ALU.mult, [k("beta")], [k("negb")])
        s.tt("dve", v3(q64("Dg"), H), v3(IREP, H), bc(q64("gc"), 64), ALU.mult, ["g64", k("gc")], [k("Dg")])
        b = rS1(); pk = "ps%d" % b
        s.mm(s.ps[b][:, 0:HW], ONES64, q64("Dg"), True, True, ["cf", k("Dg")], [pk])
        s.act(B_["erow"], s.ps[b][:, 0:HW], AF.Exp, [pk], [k("erow")])
        s.tt("dve", v3(q64("dm"), H), v3(s.ps[b][0:64, 0:HW], H), bc(q64("gc"), 64), ALU.subtract, [pk, k("gc")], [k("dm")])
        s.tt("pool", q64("dm"), q64("dm"), NEG[d], ALU.subtract, [k("dm"), "g64"], [k("dm")])
        s.act(q64("dm"), q64("dm"), AF.Exp, [k("dm")], [k("dm")], scale=-1.0)
        s.tt("pool", q64("decs"), q64("dm"), STR[d], ALU.mult, [k("dm"), "g64"], [k("decs")])
        s.tt("pool", B_["qdT"], B_["qT"], B_["erow"], ALU.mult, [k("qT"), k("erow")], [k("qdT")])
        bK = rS1(); pkK = "ps%d" % bK
        for h in range(H):
            kh = B_["kT"][:, h * 64:(h + 1) * 64]
            s.mm(s.ps[bK][0:64, h * 64:(h + 1) * 64], kh, kh, True, True, [k("kT")], [pkK])
        bQ = rS1(); pkQ = "ps%d" % bQ
        for h in range(H):
            s.mm(s.ps[bQ][0:64, h * 64:(h + 1) * 64], B_["qT"][:, h * 64:(h + 1) * 64], B_["kT"][:, h * 64:(h + 1) * 64], True, True,
                 [k("qT"), k("kT")], [pkQ])
        s.tt("dve", v3(q64("A0"), H), v3(s.ps[bK][0:64, 0:HW], H), bc(q64("negb"), 64), ALU.mult, [pkK, k("negb")], [k("A0")])
        s.tt("pool", q64("A0"), q64("A0"), q64("decs"), ALU.mult, [k("A0"), k("decs")], [k("A0")])
        s.tt("dve", q64("QK"), s.ps[bQ][0:64, 0:HW], q64("dm"), ALU.mult, [pkQ, k("dm")], [k("QK")])
        bB = rS1(); pkB = "ps%d" % bB
        for h in range(H):
            s.mm(s.ps[bB][0:64, h * 64:(h + 1) * 64], q64("A0")[:, h * 64:(h + 1) * 64], I64, True, True, [k("A0"), "cf"], [pkB])
        s.copy("act", q64("B0"), s.ps[bB][0:64, 0:HW], [pkB], [k("B0")])
        bT = rS1(); pkT = "ps%d" % bT
        for h in range(H):
            s.mm(s.ps[bT][0:64, h * 64:(h + 1) * 64], q64("QK")[:, h * 64:(h + 1) * 64], I64, True, True, [k("QK"), "cf"], [pkT])
        s.copy("dve", q64("QKT"), s.ps[bT][0:64, 0:HW], [pkT], [k("QKT")])
        s.tt("pool", q64("T0"), IREP, q64("B0"), ALU.add, ["g64", k("B0")], [k("T0")])
        A, Bm, T = "A0", "B0", "T0"
        for j in range(5):
            A2 = "A1" if A == "A0" else "A0"; B2 = "B1" if Bm == "B0" else "B0"; T2 = "T1" if T == "T0" else "T0"
            bA = rS1(); pkA = "ps%d" % bA
            for h in range(H):
                sl = slice(h * 64, (h + 1) * 64)
                s.mm(s.ps[bA][0:64, sl], q64(Bm)[:, sl], q64(A)[:, sl], True, True, [k(A), k(Bm)], [pkA])
            s.copy("act", q64(A2), s.ps[bA][0:64, 0:HW], [pkA], [k(A2)])
            if j < 4:
                bB = rS1(); pkB = "ps%d" % bB
                for h in range(H):
                    sl = slice(h * 64, (h + 1) * 64)
                    s.mm(s.ps[bB][0:64, sl], q64(A)[:, sl], q64(Bm)[:, sl], True, True, [k(A), k(Bm)], [pkB])
                s.copy("dve", q64(B2), s.ps[bB][0:64, 0:HW], [pkB], [k(B2)])
            bT = rS1(); pkT = "ps%d" % bT
            for h in range(H):
                sl = slice(h * 64, (h + 1) * 64)
                s.mm(s.ps[bT][0:64, sl], q64(A2)[:, sl], q64(T)[:, sl], True, True, [k(A2), k(T)], [pkT])
            s.tt("dve", q64(T2), q64(T), s.ps[bT][0:64, 0:HW], ALU.add, [k(T), pkT], [k(T2)])
            A, Bm, T = A2, B2, T2
        s.tt("dve", v3(q64("vb"), H), v3(q64("v"), H), bc(q64("beta"), 128), ALU.mult, [k("v"), k("beta")], [k("vb")])
        s.tt("dve", v3(q64("kbe"), H), v3(q64("k"), H), bc(q64("bk"), 128), ALU.mult, [k("k"), k("bk")], [k("kbe")])
        s.tt("dve", v3(q64("kd"), H), v3(q64("k"), H), bc(q64("ekd"), 128), ALU.mult, [k("k"), k("ekd")], [k("kd")])
        nb_ = (HD + 511) // 512
        for bb in range(nb_):
            b = rS1(); pk = "ps%d" % b
            for h in range(bb * 4, min(H, bb * 4 + 4)):
                s.mm(s.ps[b][0:64, (h % 4) * 128:(h % 4 + 1) * 128], q64(T)[:, h * 64:(h + 1) * 64], q64("vb")[:, h * 128:(h + 1) * 128],
                     True, True, [k(T), k("vb")], [pk])
            wcols = min(512, HD - bb * 512)
            s.copy("act", q64("u")[:, bb * 512: bb * 512 + wcols], s.ps[b][0:64, 0:wcols], [pk], [k("u")])
        b = rS1(); pk = "ps%d" % b
        for h in range(H):
            s.mm(s.ps[b][:, h * 64:(h + 1) * 64], q64("kbe")[:, h * 128:(h + 1) * 128], q64(T)[:, h * 64:(h + 1) * 64], True, True,
                 [k("kbe"), k(T)], [pk])
        s.copy("dve", B_["wT"], s.ps[b][:, 0:HW], [pk], [k("wT")])

    def scan(d, t0, store):
        B_ = D[d]
        k = lambda nm: K_(d, nm)
        q64 = lambda nm: B_[nm][0:64]
        Sk = "S%d" % d
        nb_ = (HD + 511) // 512
        for bb in range(nb_):
            b = bb; pk = "ps%d" % b
            for h in range(bb * 4, min(H, bb * 4 + 4)):
                s.mm(s.ps[b][0:64, (h % 4) * 128:(h % 4 + 1) * 128], B_["wT"][:, h * 64:(h + 1) * 64], B_["S"][:, h * 128:(h + 1) * 128],
                     True, True, [k("wT"), Sk], [pk])
            wcols = min(512, HD - bb * 512)
            s.tt("dve", q64("vnew")[:, bb * 512:bb * 512 + wcols], q64("u")[:, bb * 512:bb * 512 + wcols], s.ps[b][0:64, 0:wcols],
                 ALU.subtract, [k("u"), pk], [k("vnew") + str(bb)])
        for bb in range(nb_):
            b = 2 + bb; pk = "ps%d" % b
            for h in range(bb * 4, min(H, bb * 4 + 4)):
                osl = s.ps[b][0:64, (h % 4) * 128:(h % 4 + 1) * 128]
                s.mm(osl, B_["qdT"][:, h * 64:(h + 1) * 64], B_["S"][:, h * 128:(h + 1) * 128], True, False, [k("qdT"), Sk], [pk])
                s.mm(osl, q64("QKT")[:, h * 64:(h + 1) * 64], q64("vnew")[:, h * 128:(h + 1) * 128], False, True,
                     [k("QKT"), k("vnew") + str(bb)], [pk])
            wcols = min(512, HD - bb * 512)
            if store:
                s.copy("act", q64("o")[:, bb * 512:bb * 512 + wcols], s.ps[b][0:64, 0:wcols], [pk], [k("o") + str(bb)])
        if store:
            s.dma("act", "st", s.d_of[d, t0:t0 + 64, :], q64("o"), [k("o") + str(bb) for bb in range(nb_)], ["gof"])
        for bb in range(nb_):
            b = bb; pk = "ps%d" % b
            for h in range(bb * 4, min(H, bb * 4 + 4)):
                s.mm(s.ps[b][:, (h % 4) * 128:(h % 4 + 1) * 128], q64("kd")[:, h * 128:(h + 1) * 128], q64("vnew")[:, h * 128:(h + 1) * 128],
                     True, True, [k("kd"), k("vnew") + str(bb)], [pk])
            for h in range(bb * 4, min(H, bb * 4 + 4)):
                Sh = B_["S"][:, h * 128:(h + 1) * 128]
                s.stt(Sh, Sh, B_["eglast"][:, h:h + 1], s.ps[b][:, (h % 4) * 128:(h % 4 + 1) * 128], ALU.mult, ALU.add,
                      [Sk, k("eglast"), pk], [Sk])

    ncc = CTX // 64; nlc = SEQ // 64
    ch_f = [(i * 64, need_ctx) for i in range(ncc)] + [(CTX + i * 64, True) for i in range(nlc)]
    ch_b = [(i * 64, need_ctx) for i in reversed(range(ncc))] + [(CTX + i * 64, True) for i in reversed(range(nlc))]
    for i in range(len(ch_f)):
        stage1(0, ch_f[i][0]); stage1(1, ch_b[i][0])
        scan(0, ch_f[i][0], ch_f[i][1]); scan(1, ch_b[i][0], ch_b[i][1])
    if s.dbg and s.dbg.get("gstop") == "scan":
        s.reset(); return
    s.reset()
    ofb = [s.buf([HD]) for _ in range(2)]
    obb = [s.buf([HD]) for _ in range(2)]
    sqb = [s.buf([HD]) for _ in range(2)]
    ssb = [s.buf([H]) for _ in range(2)]
    zb = [s.buf([H, 128], BF16) for _ in range(2)]
    outb = [s.buf([H, 128], BF16) for _ in range(2)]
    rF = Rot(range(8))
    it = 0
    for t0 in range(0 if need_ctx else CTX, NT, 128):
        i = it % 2; it += 1
        s.dma("sp", "ld", ofb[i], s.d_of[0, t0:t0 + 128, :], ["gof"], ["fa%d" % i])
        s.dma("sp", "ld", obb[i], s.d_of[1, t0:t0 + 128, :], ["gof"], ["fb%d" % i])
        s.dma("sp", "ld", zb[i], s.d_zs[:, :, t0:t0 + 128].rearrange("h d t -> d h t"), ["zs"], ["fz%d" % i])
        s.tt("dve", ofb[i], ofb[i], obb[i], ALU.add, ["fa%d" % i, "fb%d" % i], ["fa%d" % i])
        s.act(sqb[i], ofb[i], AF.Square, ["fa%d" % i], ["fs%d" % i])
        s.S.emit("dve", "dve", (lambda e, o=ssb[i], a=v3(sqb[i], H): e.tensor_reduce(out=o, in_=a, axis=AX.X, op=ALU.add)),
                 ["fs%d" % i], ["fss%d" % i], False)
        s.act(ssb[i], ssb[i], AF.Sqrt, ["fss%d" % i, "kcol"], ["fss%d" % i], bias=s.kcol[:, 0:1], scale=1.0 / 128)
        s.recip(ssb[i], ssb[i], ["fss%d" % i], ["fss%d" % i])
        s.tt("dve", v3(ofb[i], H), v3(ofb[i], H), bc(ssb[i], 128), ALU.mult, ["fa%d" % i, "fss%d" % i], ["fa%d" % i])
        for bb in range((H + 3) // 4):
            b = rF(); pk = "ps%d" % b
            hs = list(range(bb * 4, min(H, bb * 4 + 4)))
            for h in hs:
                s.S.emit("pe", "pe", (lambda e, o=s.ps[b][:, (h % 4) * 128:(h % 4 + 1) * 128], a=ofb[i][:, h * 128:(h + 1) * 128]:
                                      e.transpose(out=o, in_=a, identity=s.ident)), ["fa%d" % i, "cf"], [pk], False)
            nh = len(hs)
            s.stt(outb[i][:, bb * 4:bb * 4 + nh, :], v3(s.ps[b][:, 0:nh * 128], nh), s.vec[:, c.v_bnorm:c.v_bnorm + 1],
                  zb[i][:, bb * 4:bb * 4 + nh, :], ALU.mult, ALU.mult, [pk, "vec", "fz%d" % i], ["fo%d" % i])
        s.dma("act", "st", s.d_oT[1, :, :, t0:t0 + 128].rearrange("h d t -> d h t"), outb[i], ["fo%d" % i], ["oT"])
    s.reset()


def host_consts(c):
    H, G = c.H, c.G
    ident = np.eye(128, dtype=np.float32)
    ones = np.ones((128, 128), np.float32)
    perm = np.zeros((128, 128), np.float32)
    for m in range(128):
        half = (m % 64) // 32
        k = m + 32 if half == 0 else m - 32
        perm[k, m] = 1.0
    cf32 = np.concatenate([ident, ones, perm], axis=1)
    j = np.arange(128)[:, None]; i = np.arange(128)[None, :]
    mprev = (j >= i).astype(np.float32); mnext = (j <= i).astype(np.float32)
    maskA = np.concatenate([np.tile(mprev, (1, G)), np.tile(mnext, (1, G))], axis=1).astype(ml_dtypes.bfloat16)
    t = np.arange(c.SEQ)
    row = (t // c.GW).astype(np.float32); col = (t % c.GW).astype(np.float32)
    inv = (1.0 / (np.float32(10000.0) ** (np.arange(0, 64, 2, dtype=np.float32) / np.float32(64)))).astype(np.float32)
    rope = np.zeros((2, 128, c.SEQ), np.float32)
    for p in range(128):
        axis = p // 64; half = (p % 64) // 32; f = p % 32
        ang = (row if axis == 0 else col) * inv[f]
        rope[0, p] = np.cos(ang)
        rope[1, p] = np.sin(ang) * (-1.0 if half == 0 else 1.0)
    jj = np.arange(64)[:, None]; cc = np.arange(64)[None, :]
    tri_f = (jj <= cc).astype(np.float32); tri_b = (jj >= cc).astype(np.float32)
    irep = np.tile(np.eye(64, dtype=np.float32), (1, H))
    neg_f = np.tile(np.where(cc <= jj, 0.0, -BIG).astype(np.float32), (1, H))
    neg_b = np.tile(np.where(cc >= jj, 0.0, -BIG).astype(np.float32), (1, H))
    str_f = np.tile((cc < jj).astype(np.float32), (1, H))
    str_b = np.tile((cc > jj).astype(np.float32), (1, H))
    g64 = np.concatenate([tri_f, tri_b, irep, neg_f, neg_b, str_f, str_b], axis=1)
    return dict(cf32=cf32, maskA=maskA, rope=rope, g64=g64)


def pack_cols(v):
    v = np.asarray(v, np.float32)
    lead = v.shape[:-1]
    K = v.shape[-1] // 128
    return np.moveaxis(v.reshape(lead + (K, 128)), -1, 0)


def host_layer_vecs(c, inp):
    L = c.L
    vecs = np.zeros((L, 128, c.NV), np.float32)
    rows = np.zeros((L, 128, c.NR), np.float32)
    H, KC = c.H, c.KC
    for l in range(L):
        vecs[l, :, c.v_npre:c.v_npre + 3 * KC] = pack_cols(inp["norm_pre"][l]).reshape(128, 3 * KC)
        vecs[l, :, c.v_npost:c.v_npost + 3 * KC] = pack_cols(inp["norm_post"][l]).reshape(128, 3 * KC)
        vecs[l, :, c.v_abias:c.v_abias + 9 * KC] = pack_cols(inp["ada_bias"][l].reshape(9, c.D)).reshape(128, 9 * KC)
        vecs[l, :, c.v_bconv:c.v_bconv + 9 * H] = pack_cols(inp["b_conv"][l]).reshape(128, 9 * H)
        vecs[l, :, c.v_bnorm] = inp["b_norm"][l]
        vecs[l, :, c.v_cq] = inp["c_qnorm"][l]
        vecs[l, :, c.v_ck] = inp["c_knorm"][l]
        r = np.concatenate([inp["a_sink"][l].reshape(-1), inp["b_A_log"][l].reshape(-1), inp["b_dt_bias"][l].reshape(-1)])
        rows[l] = np.broadcast_to(r[None, :], (128, c.NR))
    return vecs, rows


def prep_core(c, inp, b, shared):
    xT = np.ascontiguousarray(np.concatenate([inp["ctx"][b], inp["x"][b]], axis=0).T.astype(np.float32))
    cvec = np.stack([pack_cols(inp["c"][b]), pack_cols(inp["c_ctx"])], axis=-1).reshape(128, c.KC * 2)
    m = dict(shared)
    m["xT"] = xT
    m["cvec"] = np.ascontiguousarray(cvec.astype(np.float32))
    return m


def shared_inputs(c, inp):
    sh = host_consts(c)
    vecs, rows = host_layer_vecs(c, inp)
    sh["vecs"] = vecs; sh["rows"] = rows
    for k in ("ada_down", "ada_up", "ffn_wgu", "ffn_wd", "w_in", "w_br", "w_out"):
        sh[k] = np.asarray(inp[k], np.float32)
    return sh


_CACHE = {}


def kernel(**inputs):
    L = 4
    c = Cfg(L=1, PERLAYER=True)
    inp = {k: np.asarray(v) for k, v in inputs.items()}
    if "b" not in _CACHE:
        _CACHE["b"] = Builder(c)
    bld = _CACHE["b"]
    B = inp["x"].shape[0]
    ncore = B
    consts = host_consts(c)
    cfull = Cfg(L=L)
    vecs, rows = host_layer_vecs(cfull, inp)
    xT = [np.ascontiguousarray(np.concatenate([inp["ctx"][b], inp["x"][b]], axis=0).T.astype(np.float32)) for b in range(B)]
    cvec = [np.ascontiguousarray(np.stack([pack_cols(inp["c"][b]), pack_cols(inp["c_ctx"])], axis=-1).reshape(128, c.KC * 2).astype(np.float32))
            for b in range(B)]
    for l in range(L):
        sh = dict(consts)
        sh["vecs"] = vecs[l:l + 1]; sh["rows"] = rows[l:l + 1]
        for k in ("ada_down", "ada_up", "ffn_wgu", "ffn_wd", "w_in", "w_br", "w_out"):
            sh[k] = np.asarray(inp[k][l:l + 1], np.float32)
        in_maps = []
        for b in range(ncore):
            m = dict(sh); m["xT"] = xT[b]; m["cvec"] = cvec[b]
            in_maps.append(m)
        res = run_bass_kernel_spmd(bld.nc, in_maps, core_ids=list(range(ncore)))
        xT = [np.ascontiguousarray(res.results[b]["outT"]) for b in range(ncore)]
    out = np.stack([np.ascontiguousarray(xT[b][:, c.CTX:].T) for b in range(B)], axis=0)
    return out.astype(np.float32)
```

```python
import math
from contextlib import ExitStack
import numpy as np
import ml_dtypes
import concourse.bass as bass
import concourse.mybir as mybir
from concourse.bass_utils import run_bass_kernel_spmd

F32 = mybir.dt.float32
BF16 = mybir.dt.bfloat16
AF = mybir.ActivationFunctionType
ALU = mybir.AluOpType
AX = mybir.AxisListType
EPS = 1e-6
BIG = 1.0e4


class Cfg:
    def __init__(s, **kw):
        s.D = 4096; s.SEQ = 4096; s.CTX = 256; s.L = 4; s.GW = 64; s.H = 8; s.KV = 2
        s.DFF = 3072; s.RANK = 256; s.TT = 1024; s.AW = 52800; s.PERLAYER = False
        s.__dict__.update(kw)
        s.KC = s.D // 128; s.NT = s.CTX + s.SEQ; s.FC = s.DFF // 128; s.G = s.H // s.KV
        s.BW = s.H * 128; s.RC = s.RANK // 128
        o = 0
        s.o_aq = o; o += s.H * 128
        s.o_ak = o; o += s.KV * 128
        s.o_av = o; o += s.KV * 128
        s.o_bqkv = o; o += 3 * s.H * 128
        s.o_bz = o; o += s.H * 128
        s.o_bg = o; o += 4 * s.H
        s.o_cq = o; o += s.H * 128
        s.o_ck = o; o += s.KV * 128
        s.o_cv = o; o += s.KV * 128
        s.o_mg = o; o += 3 * s.D
        s.INW = o
        KC, H = s.KC, s.H
        s.v_npre = 0; s.v_npost = 3 * KC; s.v_abias = 6 * KC; s.v_bconv = 15 * KC
        s.v_bnorm = 15 * KC + 9 * H; s.v_cq = s.v_bnorm + 1; s.v_ck = s.v_bnorm + 2
        s.NV = s.v_bnorm + 3
        s.NR = 5 * H
        s.tiles = []
        for c0 in range(0, s.CTX, s.TT):
            s.tiles.append((c0, min(s.TT, s.CTX - c0), True))
        for c0 in range(0, s.SEQ, s.TT):
            s.tiles.append((s.CTX + c0, min(s.TT, s.SEQ - c0), False))


class Op:
    __slots__ = ("eng", "chan", "fn", "sig", "waits", "dma", "idx", "clock", "semval")


class Sched:
    ENGS = ("pe", "act", "dve", "pool", "sp")

    def __init__(s):
        s.streams = {e: [] for e in s.ENGS}
        s.clock = {e: {} for e in s.ENGS}
        s.nidx = {}
        s.lastw = {}
        s.readers = {}
        s.lastreal = {}
        s.dcount = {}

    NSUB = 8

    def emit(s, eng, chan, fn, reads, writes, dma):
        prev = None
        if dma:
            k = s.dcount.get(chan, 0); s.dcount[chan] = k + 1
            chan = "%s.%d" % (chan, k % s.NSUB)
            prev = s.lastreal.get(chan)
        op = Op(); op.eng = eng; op.chan = chan; op.fn = fn; op.sig = dma; op.dma = dma
        op.waits = []
        ck = s.clock[eng]

        def need(p, raw):
            if p is None:
                return
            if (not p.dma) and p.eng == eng and not raw:
                return
            if ck.get(p.chan, -1) >= p.idx:
                return
            op.waits.append(p); p.sig = True
            for c, i in p.clock.items():
                if ck.get(c, -1) < i:
                    ck[c] = i
            ck[p.chan] = p.idx

        need(prev, True)
        for k in reads:
            need(s.lastw.get(k), True)
            if k.startswith("ps"):
                rd = s.readers.get(k)
                if rd:
                    for r in rd.values():
                        if r.eng != eng:
                            need(r, False)
        for k in writes:
            need(s.lastw.get(k), False)
            rd = s.readers.get(k)
            if rd:
                for r in rd.values():
                    need(r, False)
        op.idx = s.nidx.get(chan, 0); s.nidx[chan] = op.idx + 1
        op.clock = dict(ck)
        for k in reads:
            s.readers.setdefault(k, {})[chan] = op
        for k in writes:
            s.lastw[k] = op; s.readers[k] = {}
        s.streams[eng].append(op)
        if fn is not None:
            s.lastreal[chan] = op
        return op

    def barrier(s):
        lasts = list(s.lastreal.values())
        for e in s.ENGS:
            op = Op(); op.eng = e; op.chan = e; op.fn = None; op.sig = False; op.dma = False; op.waits = []
            ck = s.clock[e]
            for p in lasts:
                if (not p.dma) and p.eng == e:
                    continue
                if ck.get(p.chan, -1) >= p.idx:
                    continue
                op.waits.append(p); p.sig = True
                for c, i in p.clock.items():
                    if ck.get(c, -1) < i:
                        ck[c] = i
                ck[p.chan] = p.idx
            op.idx = -1; op.clock = {}
            if op.waits:
                s.streams[e].append(op)

    def replay(s, nc):
        cnt = {}
        for e in s.ENGS:
            for op in s.streams[e]:
                if op.dma:
                    op.semval = 16 * (op.idx + 1)
                elif op.sig:
                    cnt[op.chan] = cnt.get(op.chan, 0) + 1
                    op.semval = cnt[op.chan]
        chans = sorted(s.nidx.keys())
        bname = {"pe": "tensor", "act": "scalar", "dve": "vector", "pool": "gpsimd", "sp": "sync"}
        with ExitStack() as es:
            sems = {c: es.enter_context(nc.semaphore("sem_" + c.replace(".", "_"))) for c in chans}
            block = es.enter_context(nc.Block())
            for e in s.ENGS:
                ops = s.streams[e]

                def body(eng, ops=ops):
                    for op in ops:
                        for p in op.waits:
                            eng.wait_ge(sems[p.chan], p.semval)
                        if op.fn is None:
                            continue
                        ins = op.fn(eng)
                        if op.sig:
                            ins.then_inc(sems[op.chan], 16 if op.dma else 1)

                getattr(block, bname[e])(body)


class Rot:
    def __init__(s, items):
        s.items = list(items); s.i = 0

    def __call__(s):
        x = s.items[s.i % len(s.items)]; s.i += 1
        return x


class Builder:
    def __init__(s, cfg, dbg=None):
        s.c = cfg
        s.dbg = dbg
        s.nc = bass.Bass("TRN2", target_bir_lowering=False)
        s.S = Sched()
        s.uid = 0
        s.dumped = set()
        s.marks = []
        c = cfg
        nc = s.nc
        L = c.L

        def inp(name, shape, dt=F32):
            return nc.dram_tensor(name, list(shape), dt, kind="ExternalInput").ap()

        def scr(name, shape, dt=F32):
            kind = "ExternalOutput" if (dbg and name in dbg) else "Internal"
            return nc.dram_tensor(name, list(shape), dt, kind=kind).ap()

        s.d_xT = inp("xT", [c.D, c.NT])
        s.d_cvec = inp("cvec", [128, c.KC * 2])
        s.d_vecs = inp("vecs", [L, 128, c.NV])
        s.d_rows = inp("rows", [L, 128, c.NR])
        s.d_cf = inp("cf32", [128, 3 * 128])
        s.d_maskA = inp("maskA", [128, 2 * c.G * 128], BF16)
        s.d_rope = inp("rope", [2, 128, c.SEQ])
        s.d_g64 = inp("g64", [64, 2 * 64 + 5 * c.H * 64])
        s.d_adown = inp("ada_down", [L, c.D, c.RANK])
        s.d_aup = inp("ada_up", [L, c.RANK, 9 * c.D])
        s.d_wgu = inp("ffn_wgu", [L, 2, 128, c.KC * 2 * c.DFF])
        s.d_wd = inp("ffn_wd", [L, 2, 128, c.FC * c.D])
        s.d_win = inp("w_in", [L, 128, c.KC * c.INW])
        s.d_wbr = inp("w_br", [L, 3, 128, c.H * c.D])
        s.d_wout = inp("w_out", [L, 128, c.KC * c.D])
        s.d_out = nc.dram_tensor("outT", [c.D, c.NT if c.PERLAYER else c.SEQ], F32, kind="ExternalOutput").ap()
        s.d_hT = scr("hT", [c.D, c.NT])
        s.d_yT = scr("yT", [c.D, c.NT])
        s.d_qa = scr("qa", [c.H, 128, c.NT], BF16)
        s.d_ka = scr("ka", [c.KV, 128, c.NT], BF16)
        s.d_va = scr("va", [c.NT, c.KV * 128], BF16)
        s.d_qc = scr("qc", [c.H, 128, c.NT], BF16)
        s.d_kc = scr("kc", [c.KV, 128, c.NT], BF16)
        s.d_vc = scr("vc", [c.NT, c.KV * 128], BF16)
        s.d_braw = scr("braw", [3 * c.H, 128, c.NT])
        s.d_zs = scr("zs", [c.H, 128, c.NT], BF16)
        s.d_gtok = scr("gtok", [c.NT, 4 * c.H])
        s.d_mg = scr("mg", [3, c.D, c.NT], BF16)
        s.d_oT = scr("oT", [3, c.H, 128, c.NT], BF16)
        s.d_gq = scr("gq", [c.H, 128, c.NT])
        s.d_gk = scr("gk", [c.H, 128, c.NT])
        s.d_gkt = scr("gkt", [c.NT, c.H * 128])
        s.d_gvt = scr("gvt", [c.NT, c.H * 128])
        s.d_of = scr("gof", [2, c.NT, c.H * 128])
        s.arena = nc.alloc_sbuf_tensor("arena", [128, c.AW], F32)
        s.arena_base = nc._sbuf_addr_for_side(None) - c.AW * 4
        s.nbuf = 0
        s.aoff = 0
        s.ps = [nc.alloc_psum_tensor("ps%d" % i, [128, 512], F32) for i in range(8)]
        s.build()

    def buf(s, shape, dt=F32):
        n = int(np.prod(shape))
        words = n if dt == F32 else (n + 1) // 2
        words = (words + 7) // 8 * 8
        assert s.aoff + words <= s.c.AW, ("arena overflow", s.aoff, words)
        s.nbuf += 1
        t = s.nc.alloc_sbuf_tensor_at("b%d" % s.nbuf, [128, n], dt, offset=s.arena_base + s.aoff * 4)
        s.aoff += words
        ap = t[:, 0:n]
        if len(shape) == 2:
            ap = ap.rearrange("p (a b) -> p a b", a=shape[0])
        elif len(shape) == 3:
            ap = ap.rearrange("p (a b c) -> p a b c", a=shape[0], b=shape[1])
        return ap

    def dump(s, name, ap, keys, dt=F32):
        if not (s.dbg and s.dbg.get("dumps")):
            return
        if name in s.dumped:
            return
        s.dumped.add(name)
        d = s.nc.dram_tensor("dump_" + name, list(ap.shape), dt, kind="ExternalOutput").ap()
        s.dma("sp", "dump", d, ap, keys, ["dump_" + name])

    def reset(s, mark=None):
        s.S.barrier()
        if s.dbg is not None:
            s.marks.append({e: len(v) for e, v in s.S.streams.items()})
        s.aoff = s.base if mark is None else mark

    def key(s, base):
        s.uid += 1
        return "%s#%d" % (base, s.uid)

    def mm(s, out, lhsT, rhs, start, stop, reads, writes):
        return s.S.emit("pe", "pe", lambda e: e.matmul(out, lhsT, rhs, start=start, stop=stop),
                        reads, writes, False)

    def act(s, out, in_, func, reads, writes, bias=None, scale=1.0):
        if bias is None:
            fn = lambda e: e.activation(out=out, in_=in_, func=func, scale=scale)
        else:
            fn = lambda e: e.activation(out=out, in_=in_, func=func, bias=bias, scale=scale)
        return s.S.emit("act", "act", fn, reads, writes, False)

    def tt(s, eng, out, a, b, op, reads, writes):
        return s.S.emit(eng, eng, lambda e: e.tensor_tensor(out=out, in0=a, in1=b, op=op), reads, writes, False)

    def ts(s, eng, out, a, s1, op0, reads, writes, s2=None, op1=None):
        if op1 is None:
            fn = lambda e: e.tensor_scalar(out=out, in0=a, scalar1=s1, scalar2=None, op0=op0)
        else:
            fn = lambda e: e.tensor_scalar(out=out, in0=a, scalar1=s1, scalar2=s2, op0=op0, op1=op1)
        return s.S.emit(eng, eng, fn, reads, writes, False)

    def stt(s, out, in0, scalar, in1, op0, op1, reads, writes):
        return s.S.emit("dve", "dve", lambda e: e.scalar_tensor_tensor(out=out, in0=in0, scalar=scalar, in1=in1,
                                                                      op0=op0, op1=op1), reads, writes, False)

    def copy(s, eng, out, in_, reads, writes):
        if eng == "act":
            return s.act(out, in_, AF.Copy, reads, writes)
        return s.S.emit(eng, eng, lambda e: e.tensor_copy(out=out, in_=in_), reads, writes, False)

    def recip(s, out, in_, reads, writes):
        return s.S.emit("dve", "dve", lambda e: e.reciprocal(out=out, in_=in_), reads, writes, False)

    def memset(s, eng, ap, val, writes):
        return s.S.emit(eng, eng, lambda e: e.memset(ap, val), [], writes, False)

    def dma(s, q, chan, out, in_, reads, writes):
        return s.S.emit(q, chan, lambda e: e.dma_start(out=out, in_=in_), reads, writes, True)

    def build(s):
        c = s.c
        s.cf = s.buf([3, 128]); s.ident = s.cf[:, 0, :]; s.ones = s.cf[:, 1, :]; s.permR = s.cf[:, 2, :]
        s.onesb = s.buf([128], BF16)
        s.maskA = s.buf([2, c.G * 128], BF16)
        s.cvec = s.buf([c.KC, 2]); s.sc = s.buf([c.KC, 2])
        s.vec = s.buf([c.NV]); s.rowsb = s.buf([c.NR])
        s.mod = s.buf([9, c.KC, 2]); s.Amod = s.buf([3, c.KC, 2]); s.Cmod = s.buf([3, c.KC, 2])
        s.esink = s.buf([c.H]); s.negA = s.buf([2 * c.H])
        s.kcol = s.buf([4])
        s.base = s.aoff
        S = s.S
        s.dma("sp", "ld", s.cf.rearrange("p a b -> p (a b)"), s.d_cf, [], ["cf"])
        s.dma("sp", "ld", s.maskA.rearrange("p a b -> p (a b)"), s.d_maskA, [], ["maskA"])
        s.dma("sp", "ld", s.cvec.rearrange("p a b -> p (a b)"), s.d_cvec, [], ["cvec"])
        s.memset("dve", s.kcol[:, 0:1], EPS, ["kcol"])
        s.memset("dve", s.kcol[:, 1:2], 1.0, ["kcol"])
        s.memset("dve", s.onesb, 1.0, ["onesb"])
        s.act(s.sc, s.cvec, AF.Silu, ["cvec"], ["sc"])
        rows = 128
        RB = min(512, c.D)
        for r0 in range(0, c.D, RB):
            s.dma("sp", "ld", s.d_hT[r0:r0 + RB, :], s.d_xT[r0:r0 + RB, :], [], ["hT"])
        stop = s.dbg.get("stop") if s.dbg else None
        for l in range(c.L):
            s.params(l)
            need_ctx = (l < c.L - 1) or c.PERLAYER
            for t in c.tiles:
                s.ffn(l, 0, t)
            if stop == ("ffn0", l): break
            for t in c.tiles:
                s.inproj(l, t)
            if stop == ("inproj", l): break
            s.attn_c(l, need_ctx)
            if stop == ("attnc", l): break
            s.attn_a(l, need_ctx)
            if stop == ("attna", l): break
            s.gdn(l, need_ctx)
            if stop == ("gdn", l): break
            for t in c.tiles:
                if t[2] and not need_ctx:
                    continue
                s.merge(l, t)
            if stop == ("merge", l): break
            for t in c.tiles:
                if t[2] and not need_ctx:
                    continue
                s.ffn(l, 1, t)
        for r0 in range(0, c.D, RB):
            s.dma("sp", "out", s.d_out[r0:r0 + RB, :], s.d_hT[r0:r0 + RB, (0 if c.PERLAYER else c.CTX):c.NT], ["hT"], ["outT"])
        S.emit("sp", "sp", None, ["outT"] + list(S.lastw.keys()), [], False)
        S.replay(s.nc)

    def params(s, l):
        c = s.c
        s.reset()
        KC, RC = c.KC, c.RC
        s.dma("sp", "ld", s.vec, s.d_vecs[l], [], ["vec"])
        s.dma("sp", "ld", s.rowsb, s.d_rows[l], [], ["rows"])
        adown = s.buf([KC, c.RANK])
        s.dma("sp", "ld", adown, s.d_adown[l].rearrange("(kc p) r -> p kc r", p=128), [], ["adown"])
        rT = s.buf([RC, 2])
        psr = Rot(range(8))
        for rc in range(RC):
            b = psr(); pk = "ps%d" % b
            for kc in range(KC):
                s.mm(s.ps[b][:, 0:2], adown[:, kc, rc * 128:(rc + 1) * 128], s.sc[:, kc, :], kc == 0, kc == KC - 1,
                     ["adown", "sc"], [pk])
            s.copy("dve", rT[:, rc, :], s.ps[b][:, 0:2], [pk], ["rT"])
        aups = [s.buf([RC, c.D]) for _ in range(2)]
        for mi in range(9):
            aup = aups[mi % 2]; ak = "aup%d" % (mi % 2)
            s.dma("sp", "ld", aup, s.d_aup[l][:, mi * c.D:(mi + 1) * c.D].rearrange("(rc p) n -> p rc n", p=128), [], [ak])
            b = psr(); pk = "ps%d" % b
            for kc in range(KC):
                for rc in range(RC):
                    s.mm(s.ps[b][:, kc * 2:kc * 2 + 2], aup[:, rc, kc * 128:(kc + 1) * 128], rT[:, rc, :],
                         rc == 0, rc == RC - 1, [ak, "rT"], [pk])
            ab = s.vec[:, c.v_abias + mi * KC: c.v_abias + (mi + 1) * KC].unsqueeze(2).to_broadcast([128, KC, 2])
            s.tt("dve", s.mod[:, mi], s.ps[b][:, 0:KC * 2].rearrange("p (k t) -> p k t", t=2), ab, ALU.add,
                 [pk, "vec"], ["mod"])
        for i in range(3):
            npre = s.vec[:, c.v_npre + i * KC: c.v_npre + (i + 1) * KC].unsqueeze(2).to_broadcast([128, KC, 2])
            npost = s.vec[:, c.v_npost + i * KC: c.v_npost + (i + 1) * KC].unsqueeze(2).to_broadcast([128, KC, 2])
            s.stt(s.Amod[:, i], s.mod[:, 3 * i + 1], 1.0, npre, ALU.add, ALU.mult, ["mod", "vec"], ["Amod"])
            coef = 1.0 if i == 1 else 0.5
            s.stt(s.Cmod[:, i], s.mod[:, 3 * i + 2], coef, npost, ALU.mult, ALU.mult, ["mod", "vec"], ["Cmod"])
        s.act(s.esink, s.rowsb[:, 0:c.H], AF.Exp, ["rows"], ["esink"])
        s.act(s.negA, s.rowsb[:, c.H:3 * c.H], AF.Exp, ["rows"], ["negA0"])
        s.ts("dve", s.negA, s.negA, -1.0, ALU.mult, ["negA0"], ["negA"])
        s.dump("sc", s.sc, ["sc"]); s.dump("rT", rT, ["rT"]); s.dump("mod", s.mod, ["mod"])
        s.dump("Amod", s.Amod, ["Amod"]); s.dump("Cmod", s.Cmod, ["Cmod"]); s.dump("vec", s.vec, ["vec"])
        s.reset()

    def colgroups(s, n):
        return [(g0, min(512, n - g0)) for g0 in range(0, n, 512)]

    def rstd_from_acc(s, acc, acck, n, rstd, rk, psr, scale):
        for (g0, gw) in s.colgroups(n):
            b = psr(); pk = "ps%d" % b
            s.mm(s.ps[b][:, 0:gw], s.ones, acc[:, g0:g0 + gw], True, True, [acck, "cf"], [pk])
            s.act(rstd[:, g0:g0 + gw], s.ps[b][:, 0:gw], AF.Sqrt, [pk, "kcol"], [rk + "s"], bias=s.kcol[:, 0:1], scale=scale)
        s.recip(rstd[:, 0:n], rstd[:, 0:n], [rk + "s"], [rk])

    def norm_u(s, l, sub, tile, uT, uk, psr):
        c = s.c
        c0, n, isctx = tile
        cls = 1 if isctx else 0
        KC = c.KC
        m0 = s.aoff
        acc = s.buf([c.TT]); rstd = s.buf([c.TT])
        hb = [s.buf([c.TT]) for _ in range(2)]
        sq = [s.buf([c.TT]) for _ in range(2)]
        ak = s.key("acc"); rk = s.key("rstd")
        s.memset("pool", acc[:, 0:n], 0.0, [ak])
        for kc in range(KC):
            h = hb[kc % 2]; hk = "hb%d" % (kc % 2)
            s.dma("sp", "ld", h[:, 0:n], s.d_hT[kc * 128:(kc + 1) * 128, c0:c0 + n], ["hT"], [hk])
            q = sq[kc % 2]; qk = "sq%d" % (kc % 2)
            s.act(q[:, 0:n], h[:, 0:n], AF.Square, [hk], [qk])
            s.tt("dve", acc[:, 0:n], acc[:, 0:n], q[:, 0:n], ALU.add, [ak, qk], [ak])
        s.rstd_from_acc(acc, ak, n, rstd, rk, psr, 1.0 / c.D)
        for kc in range(KC):
            h = hb[kc % 2]; hk = "hb%d" % (kc % 2)
            s.dma("sp", "ld", h[:, 0:n], s.d_hT[kc * 128:(kc + 1) * 128, c0:c0 + n], ["hT"], [hk])
            q = sq[kc % 2]; qk = "sq%d" % (kc % 2)
            s.stt(q[:, 0:n], h[:, 0:n], s.Amod[:, sub, kc, cls:cls + 1], rstd[:, 0:n], ALU.mult, ALU.mult,
                  [hk, rk, "Amod"], [qk])
            s.act(uT[:, kc, 0:n], q[:, 0:n], AF.Identity, [qk, "mod"], [uk + str(kc)],
                  bias=s.mod[:, 3 * sub, kc, cls:cls + 1], scale=1.0)
        if not isctx:
            s.dump("rstd", rstd, [rk]); s.dump("uT", uT, [uk + str(k) for k in range(KC)], BF16)
        s.reset(m0)

    def wload(s, wbufs, wi, src, kcn, w):
        slot = wi % len(wbufs)
        wb = wbufs[slot]; wk = "wb%d" % slot
        s.dma("pool", "w%d" % slot, wb[:, 0:kcn, 0:w], src, [], [wk, wk + "u"])
        return wb, wk

    def proj_fm(s, uT, uk, kcn, n, wsrc, chunks, wbufs, psr, consume, wi0=0):
        gsz = wbufs[0].shape[2] // 128
        wi = wi0
        for i0 in range(0, len(chunks), gsz):
            grp = chunks[i0:i0 + gsz]
            col0 = grp[0][0]; wtot = sum(w for _, w in grp)
            wb, wk = s.wload(wbufs, wi, wsrc(col0, wtot), kcn, wtot); wi += 1
            off = 0
            for ci, (cc, w) in enumerate(grp):
                for (g0, gw) in s.colgroups(n):
                    b = psr(); pk = "ps%d" % b
                    for kc in range(kcn):
                        s.mm(s.ps[b][0:w, 0:gw], wb[:, kc, off:off + w], uT[:, kc, g0:g0 + gw], kc == 0, kc == kcn - 1,
                             [wk, uk + str(kc)], [pk])
                    consume(i0 + ci, b, g0, gw)
                off += w
        return wi

    def proj_tok(s, uT, uk, kcn, n, wsrc, col0, w, wbufs, wi, psr, consume):
        wb, wk = s.wload(wbufs, wi, wsrc(col0, w), kcn, w)
        for tb in range(n // 128):
            b = psr(); pk = "ps%d" % b
            for kc in range(kcn):
                s.mm(s.ps[b][:, 0:w], uT[:, kc, tb * 128:(tb + 1) * 128], wb[:, kc, 0:w], kc == 0, kc == kcn - 1,
                     [wk, uk + str(kc)], [pk])
            consume(tb, b)
        return wi + 1

    def post(s, l, sub, tile, acc2, a2k, psr):
        c = s.c
        c0, n, isctx = tile
        cls = 1 if isctx else 0
        m0 = s.aoff
        rstd = s.buf([c.TT]); rk = s.key("rstd2")
        s.rstd_from_acc(acc2, a2k, n, rstd, rk, psr, 1.0 / c.D)
        yb = [s.buf([c.TT]) for _ in range(2)]
        hb = [s.buf([c.TT]) for _ in range(2)]
        ob = [s.buf([c.TT]) for _ in range(2)]
        for kc in range(c.KC):
            i = kc % 2
            s.dma("sp", "ld", yb[i][:, 0:n], s.d_yT[kc * 128:(kc + 1) * 128, c0:c0 + n], ["yT"], ["pyb%d" % i])
            s.dma("sp", "ld", hb[i][:, 0:n], s.d_hT[kc * 128:(kc + 1) * 128, c0:c0 + n], ["hT"], ["phb%d" % i])
            s.tt("dve", yb[i][:, 0:n], yb[i][:, 0:n], rstd[:, 0:n], ALU.mult, ["pyb%d" % i, rk], ["pyb%d" % i])
            s.stt(ob[i][:, 0:n], yb[i][:, 0:n], s.Cmod[:, sub, kc, cls:cls + 1], hb[i][:, 0:n], ALU.mult, ALU.add,
                  ["pyb%d" % i, "phb%d" % i, "Cmod"], ["pob%d" % i])
            s.dma("act", "st", s.d_hT[kc * 128:(kc + 1) * 128, c0:c0 + n], ob[i][:, 0:n], ["pob%d" % i], ["hT"])
        s.reset(m0)

    def y_consume(s, c0, n, acc2, a2k, ybufs):
        c = s.c
        st = {"i": 0}

        def consume(ci, b, g0, gw):
            pk = "ps%d" % b
            slot = (ci % 2)
            yb = ybufs[slot]; yk = "yb%d" % slot
            sqb = ybufs[2 + slot]; sk = "ysq%d" % slot
            s.act(yb[:, g0:g0 + gw], s.ps[b][:, 0:gw], AF.Copy, [pk], [yk])
            s.act(sqb[:, g0:g0 + gw], s.ps[b][:, 0:gw], AF.Square, [pk], [sk])
            s.tt("dve", acc2[:, g0:g0 + gw], acc2[:, g0:g0 + gw], sqb[:, g0:g0 + gw], ALU.add, [a2k, sk], [a2k])
            if g0 + gw >= n:
                s.dma("sp", "st", s.d_yT[ci * 128:(ci + 1) * 128, c0:c0 + n], yb[:, 0:n], [yk], ["yT"])
        return consume

    def ffn(s, l, which, tile):
        c = s.c
        c0, n, isctx = tile
        sub = 0 if which == 0 else 2
        s.reset()
        psr = Rot(range(8))
        acc2 = s.buf([c.TT]); a2k = s.key("acc2")
        mpost = s.aoff
        uT = s.buf([c.KC, c.TT], BF16); uk = s.key("uT")
        aT = s.buf([c.FC, c.TT], BF16); ak = s.key("aT")
        wbufs = [s.buf([max(c.KC, c.FC), 256], BF16) for _ in range(3)]
        s.norm_u(l, sub, tile, uT, uk, psr)
        m0 = s.aoff
        sgb = [s.buf([512]) for _ in range(2)]
        wgu = s.d_wgu[l, which]
        wi = 0
        for j in range(c.FC):
            slot = wi % 3; wb = wbufs[slot]; wk = "wb%d" % slot; wi += 1
            s.dma("pool", "w%d" % slot, wb[:, 0:c.KC, 0:256],
                  wgu[:, c.KC * 256 * j: c.KC * 256 * (j + 1)].rearrange("p (kc n) -> p kc n", kc=c.KC), [], [wk, wk + "u"])
            for gi, (g0, gw) in enumerate(s.colgroups(n)):
                bg = psr(); bu = psr()
                for kc in range(c.KC):
                    s.mm(s.ps[bg][:, 0:gw], wb[:, kc, 0:128], uT[:, kc, g0:g0 + gw], kc == 0, kc == c.KC - 1,
                         [wk, uk + str(kc)], ["ps%d" % bg])
                for kc in range(c.KC):
                    s.mm(s.ps[bu][:, 0:gw], wb[:, kc, 128:256], uT[:, kc, g0:g0 + gw], kc == 0, kc == c.KC - 1,
                         [wk + "u", uk + str(kc)], ["ps%d" % bu])
                sg = sgb[gi % 2]; sk = "sg%d" % (gi % 2)
                s.act(sg[:, 0:gw], s.ps[bg][:, 0:gw], AF.Silu, ["ps%d" % bg], [sk])
                s.tt("dve", aT[:, j, g0:g0 + gw], sg[:, 0:gw], s.ps[bu][:, 0:gw], ALU.mult, [sk, "ps%d" % bu],
                     [ak + str(j)])
        s.reset(m0)
        ybufs = [s.buf([c.TT]) for _ in range(4)]
        s.memset("pool", acc2[:, 0:n], 0.0, [a2k])
        wd = s.d_wd[l, which]
        s.proj_fm(aT, ak, c.FC, n, lambda col0, w: wd[:, c.FC * col0:c.FC * (col0 + w)].rearrange("p (kc n) -> p kc n", kc=c.FC), [(i * 128, 128) for i in range(c.KC)],
                  wbufs, psr, s.y_consume(c0, n, acc2, a2k, ybufs), wi0=wi)
        s.reset(mpost)
        s.post(l, sub, tile, acc2, a2k, psr)
        s.reset()

    def inproj(s, l, tile):
        c = s.c
        c0, n, isctx = tile
        H, KV = c.H, c.KV
        s.reset()
        psr = Rot(range(8))
        uT = s.buf([c.KC, c.TT], BF16); uk = s.key("uT")
        wbufs = [s.buf([c.KC, 256], BF16) for _ in range(3)]
        s.norm_u(l, 1, tile, uT, uk, psr)
        win = s.d_win[l]
        wsrc = lambda col0, w: win[:, c.KC * col0:c.KC * (col0 + w)].rearrange("p (kc n) -> p kc n", kc=c.KC)
        TT = c.TT
        xb = [s.buf([TT]) for _ in range(2)]
        t1 = [s.buf([TT]) for _ in range(2)]
        t2 = [s.buf([TT]) for _ in range(2)]
        ob = [s.buf([TT], BF16) for _ in range(3)]
        of = [s.buf([TT]) for _ in range(2)]
        rb = [s.buf([TT]) for _ in range(2)]
        if not isctx:
            cosb = s.buf([TT]); sinb = s.buf([TT])
            l0 = c0 - c.CTX
            s.dma("sp", "ld", cosb[:, 0:n], s.d_rope[0, :, l0:l0 + n], [], ["cosb"])
            s.dma("sp", "ld", sinb[:, 0:n], s.d_rope[1, :, l0:l0 + n], [], ["sinb"])
        cnt = {"x": 0, "o": 0, "f": 0}

        def rope_store(xa, xk, g0, gw, dst, dkey, oslot):
            o = ob[oslot]; ok = "ob%d" % oslot
            if isctx:
                s.copy("pool", o[:, g0:g0 + gw], xa, [xk], [ok])
            else:
                b = psr(); pk = "ps%d" % b
                s.mm(s.ps[b][:, 0:gw], s.permR, xa, True, True, ["cf", xk], [pk])
                i = cnt["x"] % 2
                s.tt("pool", t1[i][:, g0:g0 + gw], xa, cosb[:, g0:g0 + gw], ALU.mult, [xk, "cosb"], ["t1%d" % i])
                s.tt("dve", t2[i][:, g0:g0 + gw], s.ps[b][:, 0:gw], sinb[:, g0:g0 + gw], ALU.mult, [pk, "sinb"], ["t2%d" % i])
                s.tt("pool", o[:, g0:g0 + gw], t1[i][:, g0:g0 + gw], t2[i][:, g0:g0 + gw], ALU.add,
                     ["t1%d" % i, "t2%d" % i], [ok])
            if g0 + gw >= n:
                s.dma("sp", "st", dst, o[:, 0:n], [ok], [dkey])

        def mk_qk(dstT, dkey, nrm):
            def consume(ci, b, g0, gw):
                pk = "ps%d" % b
                i = cnt["x"] % 2; cnt["x"] += 1
                x = xb[i]; xk = "xb%d" % i
                if g0 == 0:
                    cnt["o"] += 1
                oslot = cnt["o"] % 3
                s.act(x[:, g0:g0 + gw], s.ps[b][:, 0:gw], AF.Copy, [pk], [xk])
                if nrm is not None:
                    r = rb[i]; rk = "rb%d" % i
                    s.act(t1[i][:, g0:g0 + gw], s.ps[b][:, 0:gw], AF.Square, [pk], ["t1%d" % i])
                    b2 = psr(); pk2 = "ps%d" % b2
                    s.mm(s.ps[b2][:, 0:gw], s.ones, t1[i][:, g0:g0 + gw], True, True, ["cf", "t1%d" % i], [pk2])
                    s.act(r[:, g0:g0 + gw], s.ps[b2][:, 0:gw], AF.Sqrt, [pk2, "kcol"], [rk + "s"], bias=s.kcol[:, 0:1],
                          scale=1.0 / 128)
                    s.recip(r[:, g0:g0 + gw], r[:, g0:g0 + gw], [rk + "s"], [rk])
                    s.stt(x[:, g0:g0 + gw], x[:, g0:g0 + gw], s.vec[:, nrm:nrm + 1], r[:, g0:g0 + gw], ALU.mult, ALU.mult,
                          [xk, rk, "vec"], [xk])
                rope_store(x[:, g0:g0 + gw], xk, g0, gw, dstT[ci][:, c0:c0 + n], dkey, oslot)
            return consume

        def mk_plain(dst_fn, dkey, func, dt):
            def consume(ci, b, g0, gw):
                pk = "ps%d" % b
                if dt == BF16:
                    if g0 == 0:
                        cnt["o"] += 1
                    i = cnt["o"] % 3; o = ob[i]; ok = "ob%d" % i
                else:
                    if g0 == 0:
                        cnt["f"] += 1
                    i = cnt["f"] % 2; o = of[i]; ok = "of%d" % i
                s.act(o[:, g0:g0 + gw], s.ps[b][:, 0:gw], func, [pk], [ok])
                if g0 + gw >= n:
                    s.dma("sp", "st", dst_fn(ci), o[:, 0:n], [ok], [dkey])
            return consume

        def mk_tok(dst, dkey, w, dt):
            def consume(tb, b):
                pk = "ps%d" % b
                if dt == BF16:
                    cnt["o"] += 1
                    i = cnt["o"] % 3; o = ob[i]; ok = "ob%d" % i
                else:
                    cnt["f"] += 1
                    i = cnt["f"] % 2; o = of[i]; ok = "of%d" % i
                s.act(o[:, 0:w], s.ps[b][:, 0:w], AF.Copy, [pk], [ok])
                s.dma("sp", "st", dst[c0 + tb * 128: c0 + (tb + 1) * 128, :], o[:, 0:w], [ok], [dkey])
            return consume

        ch = lambda o0, k: [(o0 + i * 128, 128) for i in range(k)]
        wi = 0
        wi = s.proj_fm(uT, uk, c.KC, n, wsrc, ch(c.o_aq, H), wbufs, psr, mk_qk(s.d_qa, "qa", None), wi)
        wi = s.proj_fm(uT, uk, c.KC, n, wsrc, ch(c.o_ak, KV), wbufs, psr, mk_qk(s.d_ka, "ka", None), wi)
        wi = s.proj_tok(uT, uk, c.KC, n, wsrc, c.o_av, KV * 128, wbufs, wi, psr, mk_tok(s.d_va, "va", KV * 128, BF16))
        wi = s.proj_fm(uT, uk, c.KC, n, wsrc, ch(c.o_bqkv, 3 * H), wbufs, psr,
                       mk_plain(lambda ci: s.d_braw[ci][:, c0:c0 + n], "braw", AF.Copy, F32), wi)
        wi = s.proj_fm(uT, uk, c.KC, n, wsrc, ch(c.o_bz, H), wbufs, psr,
                       mk_plain(lambda ci: s.d_zs[ci][:, c0:c0 + n], "zs", AF.Silu, BF16), wi)
        wi = s.proj_tok(uT, uk, c.KC, n, wsrc, c.o_bg, 4 * H, wbufs, wi, psr, mk_tok(s.d_gtok, "gtok", 4 * H, F32))
        wi = s.proj_fm(uT, uk, c.KC, n, wsrc, ch(c.o_cq, H), wbufs, psr, mk_qk(s.d_qc, "qc", c.v_cq), wi)
        wi = s.proj_fm(uT, uk, c.KC, n, wsrc, ch(c.o_ck, KV), wbufs, psr, mk_qk(s.d_kc, "kc", c.v_ck), wi)
        wi = s.proj_tok(uT, uk, c.KC, n, wsrc, c.o_cv, KV * 128, wbufs, wi, psr, mk_tok(s.d_vc, "vc", KV * 128, BF16))
        wi = s.proj_fm(uT, uk, c.KC, n, wsrc, ch(c.o_mg, 3 * c.KC), wbufs, psr,
                       mk_plain(lambda ci: s.d_mg[ci // c.KC][(ci % c.KC) * 128:(ci % c.KC + 1) * 128, c0:c0 + n],
                                "mg", AF.Sigmoid, BF16), wi)
        s.reset()

    def attn_c(s, l, need_ctx):
        attn_c(s, l, need_ctx)

    def attn_a(s, l, need_ctx):
        attn_a(s, l, need_ctx)

    def gdn(s, l, need_ctx):
        gdn(s, l, need_ctx)

    def merge(s, l, tile):
        merge(s, l, tile)


def attn_c(s, l, need_ctx):
    c = s.c
    s.reset()
    NT, NCK = c.NT, c.NT // 128
    scale = 1.0 / math.sqrt(128.0)
    kT = s.buf([NT], BF16); vt = s.buf([NCK, 128], BF16)
    qb = [s.buf([512], BF16) for _ in range(2)]
    pb = [s.buf([512], BF16) for _ in range(3)]
    rd = [s.buf([512]) for _ in range(2)]
    ob = [s.buf([512], BF16) for _ in range(2)]
    rS = Rot([0, 1, 2]); rO = Rot([3, 4]); rD = Rot([5, 6])
    it = 0
    for g in range(c.KV):
        s.dma("sp", "ld", kT, s.d_kc[g], ["kc"], ["ckT"])
        s.dma("sp", "ld", vt, s.d_vc[:, g * 128:(g + 1) * 128].rearrange("(ck p) d -> p ck d", p=128), ["vc"], ["cvt"])
        for hh in range(c.G):
            h = g * c.G + hh
            qgroups = []
            if need_ctx:
                qgroups += [(q0, min(512, c.CTX - q0), c.CTX // 128) for q0 in range(0, c.CTX, 512)]
            qgroups += [(c.CTX + q0, min(512, c.SEQ - q0), NCK) for q0 in range(0, c.SEQ, 512)]
            for (q0, qw, nk) in qgroups:
                i = it % 2; it += 1
                q = qb[i]; qk = "cq%d" % i
                s.dma("sp", "ld", q[:, 0:qw], s.d_qc[h][:, q0:q0 + qw], ["qc"], [qk])
                bo = rO(); bd = rD()
                for ck in range(nk):
                    bs = rS(); pk = "ps%d" % bs
                    s.mm(s.ps[bs][:, 0:qw], kT[:, ck * 128:(ck + 1) * 128], q[:, 0:qw], True, True, ["ckT", qk], [pk])
                    pi = (ck % 3); p = pb[pi]; ppk = "cp%d" % pi
                    s.act(p[:, 0:qw], s.ps[bs][:, 0:qw], AF.Exp, [pk], [ppk], scale=scale)
                    s.mm(s.ps[bo][:, 0:qw], vt[:, ck, :], p[:, 0:qw], ck == 0, ck == nk - 1, ["cvt", ppk], ["ps%d" % bo])
                    s.mm(s.ps[bd][:, 0:qw], s.onesb, p[:, 0:qw], ck == 0, ck == nk - 1, ["onesb", ppk], ["ps%d" % bd])
                s.recip(rd[i][:, 0:qw], s.ps[bd][:, 0:qw], ["ps%d" % bd], ["crd%d" % i])
                s.tt("dve", ob[i][:, 0:qw], s.ps[bo][:, 0:qw], rd[i][:, 0:qw], ALU.mult, ["ps%d" % bo, "crd%d" % i], ["cob%d" % i])
                s.dma("pool", "st2", s.d_oT[2, h][:, q0:q0 + qw], ob[i][:, 0:qw], ["cob%d" % i], ["oT"])
    s.reset()


def attn_a(s, l, need_ctx):
    c = s.c
    s.reset()
    NT, NCK, G = c.NT, c.NT // 128, c.G
    GW = G * 128
    scale = 1.0 / math.sqrt(128.0)
    kT = s.buf([NT], BF16); vt = s.buf([NCK, 128], BF16)
    qa = s.buf([G, NT], BF16)
    pb = [s.buf([GW], BF16) for _ in range(3)]
    den = [s.buf([GW]) for _ in range(2)]
    ob = [s.buf([G, 128], BF16) for _ in range(2)]
    rS = Rot([0, 1, 2]); rO = Rot([3, 4]); rD = Rot([5, 6])
    nctx = c.CTX // 128
    nb = c.SEQ // 128
    it = 0
    for g in range(c.KV):
        s.dma("sp", "ld", kT, s.d_ka[g], ["ka"], ["akT"])
        s.dma("sp", "ld", vt, s.d_va[:, g * 128:(g + 1) * 128].rearrange("(ck p) d -> p ck d", p=128), ["va"], ["avt"])
        s.dma("sp", "ld", qa, s.d_qa[g * G:(g + 1) * G].rearrange("h d t -> d h t"), ["qa"], ["aqa"])
        blocks = []
        if need_ctx:
            for n in range(nctx):
                blocks.append((n, [(k, None) for k in range(nctx)]))
        for n in range(nb):
            ks = [(k, None) for k in range(nctx)]
            if n > 0:
                ks.append((nctx + n - 1, 0))
            ks.append((nctx + n, None))
            if n < nb - 1:
                ks.append((nctx + n + 1, 1))
            blocks.append((nctx + n, ks))
        for (qblk, ks) in blocks:
            i = it % 2; it += 1
            q = qa[:, :, qblk * 128:(qblk + 1) * 128]
            bo = rO(); bd = rD()
            for j, (ck, mk) in enumerate(ks):
                bs = rS(); pk = "ps%d" % bs
                s.mm(s.ps[bs][:, 0:GW].rearrange("p (g t) -> p g t", g=G), kT[:, ck * 128:(ck + 1) * 128], q, True, True,
                     ["akT", "aqa"], [pk])
                pi = j % 3; p = pb[pi]; ppk = "ap%d" % pi
                s.act(p, s.ps[bs][:, 0:GW], AF.Exp, [pk], [ppk], scale=scale)
                if mk is not None:
                    s.tt("pool", p, p, s.maskA[:, mk, :], ALU.mult, [ppk, "maskA"], [ppk])
                s.mm(s.ps[bo][:, 0:GW], vt[:, ck, :], p, j == 0, j == len(ks) - 1, ["avt", ppk], ["ps%d" % bo])
                s.mm(s.ps[bd][:, 0:GW], s.onesb, p, j == 0, j == len(ks) - 1, ["onesb", ppk], ["ps%d" % bd])
            d = den[i]; dk = "aden%d" % i
            for hh in range(G):
                h = g * G + hh
                s.ts("dve", d[:, hh * 128:(hh + 1) * 128], s.ps[bd][:, hh * 128:(hh + 1) * 128], s.esink[:, h:h + 1], ALU.add,
                     ["ps%d" % bd, "esink"], [dk])
            s.recip(d, d, [dk], [dk])
            s.tt("dve", ob[i].rearrange("p g t -> p (g t)"), s.ps[bo][:, 0:GW], d, ALU.mult, ["ps%d" % bo, dk], ["aob%d" % i])
            s.dma("pool", "st2", s.d_oT[0, g * G:(g + 1) * G, :, qblk * 128:(qblk + 1) * 128].rearrange("h d t -> d h t"),
                  ob[i], ["aob%d" % i], ["oT"])
    s.reset()


def merge(s, l, tile):
    c = s.c
    c0, n, isctx = tile
    H, KC = c.H, c.KC
    s.reset()
    psr = Rot(range(8))
    acc2 = s.buf([c.TT]); a2k = s.key("acc2")
    mpost = s.aoff
    mT = s.buf([KC, c.TT], BF16); mk = s.key("mT")
    wbufs = [s.buf([KC, 256], BF16) for _ in range(2)]
    m1 = s.aoff
    oT = s.buf([3 * H, c.TT], BF16)
    wbr = [s.buf([3 * H, 128], BF16) for _ in range(2)]
    gb = [s.buf([3, c.TT], BF16) for _ in range(2)]
    tm = [s.buf([512]) for _ in range(4)]
    for i in range(3):
        s.dma("sp", "ld", oT[:, i * H:(i + 1) * H, 0:n],
              s.d_oT[i][:, :, c0:c0 + n].rearrange("h d t -> d h t"), ["oT"], ["moT%d" % i])
    it = 0
    for fc in range(KC):
        wslot = fc % 2; wb = wbr[wslot]; wk = "wbr%d" % wslot
        for i in range(3):
            s.dma("pool", "wr%d" % wslot, wb[:, i * H:(i + 1) * H, :],
                  s.d_wbr[l, i][:, H * 128 * fc:H * 128 * (fc + 1)].rearrange("p (kc n) -> p kc n", kc=H), [], [wk + str(i)])
        g = gb[wslot]; gk = "mgb%d" % wslot
        s.dma("sp", "ld", g[:, :, 0:n], s.d_mg[:, fc * 128:(fc + 1) * 128, c0:c0 + n].rearrange("i p t -> p i t"), ["mg"], [gk])
        for (g0, gw) in s.colgroups(n):
            bs = []
            for i in range(3):
                b = psr(); bs.append(b)
                for k in range(H):
                    s.mm(s.ps[b][:, 0:gw], wb[:, i * H + k, :], oT[:, i * H + k, g0:g0 + gw], k == 0, k == H - 1,
                         [wk + str(i), "moT%d" % i], ["ps%d" % b])
            j = it % 2; it += 1
            ta = tm[2 * j]; tb_ = tm[2 * j + 1]; tak = "mta%d" % j; tbk = "mtb%d" % j
            s.tt("dve", ta[:, 0:gw], s.ps[bs[0]][:, 0:gw], g[:, 0, g0:g0 + gw], ALU.mult, ["ps%d" % bs[0], gk], [tak])
            s.tt("dve", tb_[:, 0:gw], s.ps[bs[1]][:, 0:gw], g[:, 1, g0:g0 + gw], ALU.mult, ["ps%d" % bs[1], gk], [tbk])
            s.tt("pool", ta[:, 0:gw], ta[:, 0:gw], tb_[:, 0:gw], ALU.add, [tak, tbk], [tak])
            s.tt("dve", tb_[:, 0:gw], s.ps[bs[2]][:, 0:gw], g[:, 2, g0:g0 + gw], ALU.mult, ["ps%d" % bs[2], gk], [tbk])
            s.tt("pool", mT[:, fc, g0:g0 + gw], ta[:, 0:gw], tb_[:, 0:gw], ALU.add, [tak, tbk], [mk + str(fc)])
    s.reset(m1)
    ybufs = [s.buf([c.TT]) for _ in range(4)]
    s.memset("pool", acc2[:, 0:n], 0.0, [a2k])
    wo = s.d_wout[l]
    s.proj_fm(mT, mk, KC, n, lambda col0, w: wo[:, KC * col0:KC * (col0 + w)].rearrange("p (kc n) -> p kc n", kc=KC), [(i * 128, 128) for i in range(KC)],
              wbufs, psr, s.y_consume(c0, n, acc2, a2k, ybufs))
    s.reset(mpost)
    s.post(l, 1, tile, acc2, a2k, psr)
    s.reset()


def gdn(s, l, need_ctx):
    c = s.c
    H, NT, CTX, SEQ = c.H, c.NT, c.CTX, c.SEQ
    HW = H * 64; HD = H * 128
    s.reset()
    psr = Rot(range(8))
    W = 512
    xin = [s.buf([W + 2]) for _ in range(2)]
    yb = [s.buf([W]) for _ in range(2)]
    sb = [s.buf([W]) for _ in range(2)]
    qb = [s.buf([W]) for _ in range(2)]
    rb = [s.buf([W]) for _ in range(2)]
    tb = [s.buf([H, 128]) for _ in range(2)]
    it = 0
    if s.dbg and s.dbg.get('exp2'):
        dmy = s.buf([8])
        s.memset("dve", dmy, 0.5, ["dmy"])
        s.act(dmy, dmy, AF.Silu, ["dmy"], ["dmy"])
    seqs = [(0, CTX), (CTX, SEQ)]
    for (s0, slen) in seqs:
        for b0 in range(0, slen, W):
            w = min(W, slen - b0)
            t0 = s0 + b0
            nblk = w // 128
            for kind in (0, 1, 2):
                for h in (list(reversed(range(H))) if s.dbg and s.dbg.get('exp1') else range(H)):
                    fc = kind * H + h
                    i = it % 2; it += 1
                    x = xin[i]; xk = "gx%d" % i
                    lo = 1 if b0 == 0 else 0
                    hi = 1 if b0 + w >= slen else 0
                    if lo:
                        s.memset("pool", x[:, 0:1], 0.0, [xk + "l"])
                    if hi:
                        s.memset("pool", x[:, w + 1:w + 2], 0.0, [xk + "h"])
                    s.dma("sp", "ld", x[:, lo:w + 2 - hi], s.d_braw[fc][:, t0 - 1 + lo: t0 + w + 1 - hi], ["braw"], [xk])
                    cw = lambda j: s.vec[:, c.v_bconv + j * 3 * H + fc: c.v_bconv + j * 3 * H + fc + 1]
                    y = yb[i]; yk = "gy%d" % i
                    s.ts("dve", y[:, 0:w], x[:, 0:w], cw(0), ALU.mult, [xk, xk + "l", "vec"], [yk])
                    s.stt(y[:, 0:w], x[:, 1:w + 1], cw(1), y[:, 0:w], ALU.mult, ALU.add, [xk, yk, "vec"], [yk])
                    s.stt(y[:, 0:w], x[:, 2:w + 2], cw(2), y[:, 0:w], ALU.mult, ALU.add, [xk, xk + "h", yk, "vec"], [yk])
                    sv = sb[i]; sk = "gs%d" % i
                    s.dump("g_x", x[:, 0:w + 2], [xk, xk + "l", xk + "h"]); s.dump("g_y", y[:, 0:w], [yk])
                    s.act(sv[:, 0:w], y[:, 0:w], AF.Silu, [yk], [sk])
                    s.dump("g_s", sv[:, 0:w], [sk])
                    if kind < 2:
                        q = qb[i]; qk = "gq%d" % i
                        s.act(q[:, 0:w], sv[:, 0:w], AF.Square, [sk], [qk])
                        b = psr(); pk = "ps%d" % b
                        s.mm(s.ps[b][:, 0:w], s.ones, q[:, 0:w], True, True, ["cf", qk], [pk])
                        r = rb[i]; rk = "gr%d" % i
                        s.act(r[:, 0:w], s.ps[b][:, 0:w], AF.Sqrt, [pk, "kcol"], [rk + "s"], bias=s.kcol[:, 0:1], scale=1.0)
                        s.recip(r[:, 0:w], r[:, 0:w], [rk + "s"], [rk])
                        if kind == 0:
                            s.stt(sv[:, 0:w], sv[:, 0:w], 1.0 / math.sqrt(128.0), r[:, 0:w], ALU.mult, ALU.mult, [sk, rk], [sk])
                            s.dma("act", "st", s.d_gq[h][:, t0:t0 + w], sv[:, 0:w], [sk], ["gq"])
                        else:
                            s.tt("dve", sv[:, 0:w], sv[:, 0:w], r[:, 0:w], ALU.mult, [sk, rk], [sk])
                            s.dma("act", "st", s.d_gk[h][:, t0:t0 + w], sv[:, 0:w], [sk], ["gk"])
                    if kind >= 1:
                        dst = s.d_gkt if kind == 1 else s.d_gvt
                        dk = "gkt" if kind == 1 else "gvt"
                        for bl in range(nblk):
                            b = psr(); pk = "ps%d" % b
                            s.S.emit("pe", "pe", (lambda e, o=s.ps[b][:, 0:128], a=sv[:, bl * 128:(bl + 1) * 128]:
                                                  e.transpose(out=o, in_=a, identity=s.ident)), [sk, "cf"], [pk], False)
                            j = (it + bl) % 2
                            tt_ = tb[j]; tk = "gt%d" % j
                            s.copy("dve", tt_[:, 0, :], s.ps[b][:, 0:128], [pk], [tk])
                            s.dma("act", "st", dst[t0 + bl * 128: t0 + (bl + 1) * 128, h * 128:(h + 1) * 128], tt_[:, 0, :], [tk], [dk])
    if s.dbg and s.dbg.get("gstop") == "prep":
        s.reset(); return
    s.reset()
    s.g64 = s.buf([2 * 64 + 5 * c.H * 64])
    s.dma("sp", "ld", s.g64[0:64, :], s.d_g64, [], ["g64"])
    G0 = 0; TRI = [s.g64[0:64, 0:64], s.g64[0:64, 64:128]]
    o0 = 128
    IREP = s.g64[0:64, o0:o0 + HW]
    NEG = [s.g64[0:64, o0 + HW:o0 + 2 * HW], s.g64[0:64, o0 + 2 * HW:o0 + 3 * HW]]
    STR = [s.g64[0:64, o0 + 3 * HW:o0 + 4 * HW], s.g64[0:64, o0 + 4 * HW:o0 + 5 * HW]]
    I64 = s.ident[0:64, 0:64]
    ONES64 = s.ones[0:64, :]
    one64 = s.kcol[0:64, 1:2]
    rS1 = Rot([4, 5, 6, 7])

    def v3(ap, a):
        return ap.rearrange("p (a b) -> p a b", a=a)

    def bc(ap, n):
        return ap.unsqueeze(2).to_broadcast([ap.shape[0], H, n])

    D = []
    for d in range(2):
        B_ = {}
        for nm, sz in (("qT", HW), ("kT", HW), ("k", HD), ("v", HD), ("gt", 4 * H), ("x0", H), ("g", H), ("beta", H),
                       ("gc", H), ("egc", H), ("ekd", H), ("bk", H), ("negb", H), ("eglast", H), ("Dg", HW), ("erow", HW),
                       ("dm", HW), ("decs", HW), ("A0", HW), ("A1", HW), ("B0", HW), ("B1", HW), ("QK", HW), ("QKT", HW),
                       ("T0", HW), ("T1", HW), ("vb", HD), ("kbe", HD), ("kd", HD), ("u", HD), ("wT", HW), ("qdT", HW),
                       ("vnew", HD), ("o", HD), ("S", HD)):
            B_[nm] = s.buf([sz])
        D.append(B_)
        s.memset("pool", B_["S"], 0.0, ["S%d" % d])

    def K_(d, nm):
        return "g%d%s" % (d, nm)

    def stage1(d, t0):
        B_ = D[d]
        k = lambda nm: K_(d, nm)
        q64 = lambda nm: B_[nm][0:64]
        s.dma("sp", "ld", v3(B_["qT"], H), s.d_gq[:, :, t0:t0 + 64].rearrange("h d t -> d h t"), ["gq"], [k("qT")])
        s.dma("sp", "ld", v3(B_["kT"], H), s.d_gk[:, :, t0:t0 + 64].rearrange("h d t -> d h t"), ["gk"], [k("kT")])
        s.dma("sp", "ld", q64("k"), s.d_gkt[t0:t0 + 64, :], ["gkt"], [k("k")])
        s.dma("sp", "ld", q64("v"), s.d_gvt[t0:t0 + 64, :], ["gvt"], [k("v")])
        s.dma("sp", "ld", q64("gt"), s.d_gtok[t0:t0 + 64, :], ["gtok"], [k("gt")])
        gt = q64("gt")
        s.tt("dve", q64("x0"), gt[:, d * 2 * H: d * 2 * H + H], s.rowsb[0:64, 3 * H + d * H: 3 * H + (d + 1) * H], ALU.add,
             [k("gt"), "rows"], [k("x0")])
        s.act(q64("x0"), q64("x0"), AF.Exp, [k("x0")], [k("x0")])
        s.act(q64("x0"), q64("x0"), AF.Ln, [k("x0"), "kcol"], [k("x0")], bias=one64, scale=1.0)
        s.tt("dve", q64("g"), q64("x0"), s.negA[0:64, d * H:(d + 1) * H], ALU.mult, [k("x0"), "negA"], [k("g")])
        s.act(q64("beta"), gt[:, d * 2 * H + H: (d + 1) * 2 * H], AF.Exp, [k("gt")], [k("beta")], scale=-1.0)
        s.ts("dve", q64("beta"), q64("beta"), 1.0, ALU.add, [k("beta")], [k("beta")])
        s.recip(q64("beta"), q64("beta"), [k("beta")], [k("beta")])
        b = rS1(); pk = "ps%d" % b
        s.mm(s.ps[b][0:64, 0:H], TRI[d], q64("g"), True, True, ["g64", k("g")], [pk])
        s.copy("dve", q64("gc"), s.ps[b][0:64, 0:H], [pk], [k("gc")])
        b2 = rS1(); pk2 = "ps%d" % b2
        s.mm(s.ps[b2][:, 0:H], ONES64, q64("g"), True, True, ["cf", k("g")], [pk2])
        s.act(B_["eglast"], s.ps[b2][:, 0:H], AF.Exp, [pk2], [k("eglast")])
        s.tt("dve", q64("ekd"), s.ps[b2][0:64, 0:H], q64("gc"), ALU.subtract, [pk2, k("gc")], [k("ekd")])
        s.act(q64("ekd"), q64("ekd"), AF.Exp, [k("ekd")], [k("ekd")])
        s.act(q64("egc"), q64("gc"), AF.Exp, [k("gc")], [k("egc")])
        s.tt("dve", q64("bk"), q64("beta"), q64("egc"), ALU.mult, [k("beta"), k("egc")], [k("bk")])
        s.ts("dve", q64("negb"), q64("beta"), -1.0, ALU.mult, [k("beta")], [k("negb")])
        s.tt("dve", v3(q64("Dg"), H), v3(IREP, H), bc(q64("gc"), 64), ALU.mult, ["g64", k("gc")], [k("Dg")])
        b = rS1(); pk = "ps%d" % b
        s.mm(s.ps[b][:, 0:HW], ONES64, q64("Dg"), True, True, ["cf", k("Dg")], [pk])
        s.act(B_["erow"], s.ps[b][:, 0:HW], AF.Exp, [pk], [k("erow")])
        s.tt("dve", v3(q64("dm"), H), v3(s.ps[b][0:64, 0:HW], H), bc(q64("gc"), 64), ALU.subtract, [pk, k("gc")], [k("dm")])
        s.tt("pool", q64("dm"), q64("dm"), NEG[d], ALU.subtract, [k("dm"), "g64"], [k("dm")])
        s.act(q64("dm"), q64("dm"), AF.Exp, [k("dm")], [k("dm")], scale=-1.0)
        s.tt("pool", q64("decs"), q64("dm"), STR[d], ALU.mult, [k("dm"), "g64"], [k("decs")])
        s.tt("pool", B_["qdT"], B_["qT"], B_["erow"], ALU.mult, [k("qT"), k("erow")], [k("qdT")])
        bK = rS1(); pkK = "ps%d" % bK
        for h in range(H):
            kh = B_["kT"][:, h * 64:(h + 1) * 64]
            s.mm(s.ps[bK][0:64, h * 64:(h + 1) * 64], kh, kh, True, True, [k("kT")], [pkK])
        bQ = rS1(); pkQ = "ps%d" % bQ
        for h in range(H):
            s.mm(s.ps[bQ][0:64, h * 64:(h + 1) * 64], B_["qT"][:, h * 64:(h + 1) * 64], B_["kT"][:, h * 64:(h + 1) * 64], True, True,
                 [k("qT"), k("kT")], [pkQ])
        s.tt("dve", v3(q64("A0"), H), v3(s.ps[bK][0:64, 0:HW], H), bc(q64("negb"), 64), ALU.mult, [pkK, k("negb")], [k("A0")])
        s.tt("pool", q64("A0"), q64("A0"), q64("decs"), ALU.mult, [k("A0"), k("decs")], [k("A0")])
        s.tt("dve", q64("QK"), s.ps[bQ][0:64, 0:HW], q64("dm"), ALU.mult, [pkQ, k("dm")], [k("QK")])
        bB = rS1(); pkB = "ps%d" % bB
        for h in range(H):
            s.mm(s.ps[bB][0:64, h * 64:(h + 1) * 64], q64("A0")[:, h * 64:(h + 1) * 64], I64, True, True, [k("A0"), "cf"], [pkB])
        s.copy("act", q64("B0"), s.ps[bB][0:64, 0:HW], [pkB], [k("B0")])
        bT = rS1(); pkT = "ps%d" % bT
        for h in range(H):
            s.mm(s.ps[bT][0:64, h * 64:(h + 1) * 64], q64("QK")[:, h * 64:(h + 1) * 64], I64, True, True, [k("QK"), "cf"], [pkT])
        s.copy("dve", q64("QKT"), s.ps[bT][0:64, 0:HW], [pkT], [k("QKT")])
        s.tt("pool", q64("T0"), IREP, q64("B0"), ALU.add, ["g64", k("B0")], [k("T0")])
        A, Bm, T = "A0", "B0", "T0"
        for j in range(5):
            A2 = "A1" if A == "A0" else "A0"; B2 = "B1" if Bm == "B0" else "B0"; T2 = "T1" if T == "T0" else "T0"
            bA = rS1(); pkA = "ps%d" % bA
            for h in range(H):
                sl = slice(h * 64, (h + 1) * 64)
                s.mm(s.ps[bA][0:64, sl], q64(Bm)[:, sl], q64(A)[:, sl], True, True, [k(A), k(Bm)], [pkA])
            s.copy("act", q64(A2), s.ps[bA][0:64, 0:HW], [pkA], [k(A2)])
            if j < 4:
                bB = rS1(); pkB = "ps%d" % bB
                for h in range(H):
                    sl = slice(h * 64, (h + 1) * 64)
                    s.mm(s.ps[bB][0:64, sl], q64(A)[:, sl], q64(Bm)[:, sl], True, True, [k(A), k(Bm)], [pkB])
                s.copy("dve", q64(B2), s.ps[bB][0:64, 0:HW], [pkB], [k(B2)])
            bT = rS1(); pkT = "ps%d" % bT
            for h in range(H):
                sl = slice(h * 64, (h + 1) * 64)
                s.mm(s.ps[bT][0:64, sl], q64(A2)[:, sl], q64(T)[:, sl], True, True, [k(A2), k(T)], [pkT])
            s.tt("dve", q64(T2), q64(T), s.ps[bT][0:64, 0:HW], ALU.add, [k(T), pkT], [k(T2)])
            A, Bm, T = A2, B2, T2
        s.tt("dve", v3(q64("vb"), H), v3(q64("v"), H), bc(q64("beta"), 128), ALU.mult, [k("v"), k("beta")], [k("vb")])
        s.tt("dve", v3(q64("kbe"), H), v3(q64("k"), H), bc(q64("bk"), 128), ALU.mult, [k("k"), k("bk")], [k("kbe")])
        s.tt("dve", v3(q64("kd"), H), v3(q64("k"), H), bc(q64("ekd"), 128), ALU.mult, [k("k"), k("ekd")], [k("kd")])
        nb_ = (HD + 511) // 512
        for bb in range(nb_):
            b = rS1(); pk = "ps%d" % b
            for h in range(bb * 4, min(H, bb * 4 + 4)):
                s.mm(s.ps[b][0:64, (h % 4) * 128:(h % 4 + 1) * 128], q64(T)[:, h * 64:(h + 1) * 64], q64("vb")[:, h * 128:(h + 1) * 128],
                     True, True, [k(T), k("vb")], [pk])
            wcols = min(512, HD - bb * 512)
            s.copy("act", q64("u")[:, bb * 512: bb * 512 + wcols], s.ps[b][0:64, 0:wcols], [pk], [k("u")])
        b = rS1(); pk = "ps%d" % b
        for h in range(H):
            s.mm(s.ps[b][:, h * 64:(h + 1) * 64], q64("kbe")[:, h * 128:(h + 1) * 128], q64(T)[:, h * 64:(h + 1) * 64], True, True,
                 [k("kbe"), k(T)], [pk])
        s.copy("dve", B_["wT"], s.ps[b][:, 0:HW], [pk], [k("wT")])

    def scan(d, t0, store):
        B_ = D[d]
        k = lambda nm: K_(d, nm)
        q64 = lambda nm: B_[nm][0:64]
        Sk = "S%d" % d
        nb_ = (HD + 511) // 512
        for bb in range(nb_):
            b = bb; pk = "ps%d" % b
            for h in range(bb * 4, min(H, bb * 4 + 4)):
                s.mm(s.ps[b][0:64, (h % 4) * 128:(h % 4 + 1) * 128], B_["wT"][:, h * 64:(h + 1) * 64], B_["S"][:, h * 128:(h + 1) * 128],
                     True, True, [k("wT"), Sk], [pk])
            wcols = min(512, HD - bb * 512)
            s.tt("dve", q64("vnew")[:, bb * 512:bb * 512 + wcols], q64("u")[:, bb * 512:bb * 512 + wcols], s.ps[b][0:64, 0:wcols],
                 ALU.subtract, [k("u"), pk], [k("vnew") + str(bb)])
        for bb in range(nb_):
            b = 2 + bb; pk = "ps%d" % b
            for h in range(bb * 4, min(H, bb * 4 + 4)):
                osl = s.ps[b][0:64, (h % 4) * 128:(h % 4 + 1) * 128]
                s.mm(osl, B_["qdT"][:, h * 64:(h + 1) * 64], B_["S"][:, h * 128:(h + 1) * 128], True, False, [k("qdT"), Sk], [pk])
                s.mm(osl, q64("QKT")[:, h * 64:(h + 1) * 64], q64("vnew")[:, h * 128:(h + 1) * 128], False, True,
                     [k("QKT"), k("vnew") + str(bb)], [pk])
            wcols = min(512, HD - bb * 512)
            if store:
                s.copy("act", q64("o")[:, bb * 512:bb * 512 + wcols], s.ps[b][0:64, 0:wcols], [pk], [k("o") + str(bb)])
        if store:
            s.dma("act", "st", s.d_of[d, t0:t0 + 64, :], q64("o"), [k("o") + str(bb) for bb in range(nb_)], ["gof"])
        for bb in range(nb_):
            b = bb; pk = "ps%d" % b
            for h in range(bb * 4, min(H, bb * 4 + 4)):
                s.mm(s.ps[b][:, (h % 4) * 128:(h % 4 + 1) * 128], q64("kd")[:, h * 128:(h + 1) * 128], q64("vnew")[:, h * 128:(h + 1) * 128],
                     True, True, [k("kd"), k("vnew") + str(bb)], [pk])
            for h in range(bb * 4, min(H, bb * 4 + 4)):
                Sh = B_["S"][:, h * 128:(h + 1) * 128]
                s.stt(Sh, Sh, B_["eglast"][:, h:h + 1], s.ps[b][:, (h % 4) * 128:(h % 4 + 1) * 128], ALU.mult, ALU.add,
                      [Sk, k("eglast"), pk], [Sk])

    ncc = CTX // 64; nlc = SEQ // 64
    ch_f = [(i * 64, need_ctx) for i in range(ncc)] + [(CTX + i * 64, True) for i in range(nlc)]
    ch_b = [(i * 64, need_ctx) for i in reversed(range(ncc))] + [(CTX + i * 64, True) for i in reversed(range(nlc))]
    for i in range(len(ch_f)):
        stage1(0, ch_f[i][0]); stage1(1, ch_b[i][0])
        scan(0, ch_f[i][0], ch_f[i][1]); scan(1, ch_b[i][0], ch_b[i][1])
    if s.dbg and s.dbg.get("gstop") == "scan":
        s.reset(); return
    s.reset()
    ofb = [s.buf([HD]) for _ in range(2)]
    obb = [s.buf([HD]) for _ in range(2)]
    sqb = [s.buf([HD]) for _ in range(2)]
    ssb = [s.buf([H]) for _ in range(2)]
    zb = [s.buf([H, 128], BF16) for _ in range(2)]
    outb = [s.buf([H, 128], BF16) for _ in range(2)]
    rF = Rot(range(8))
    it = 0
    for t0 in range(0 if need_ctx else CTX, NT, 128):
        i = it % 2; it += 1
        s.dma("sp", "ld", ofb[i], s.d_of[0, t0:t0 + 128, :], ["gof"], ["fa%d" % i])
        s.dma("sp", "ld", obb[i], s.d_of[1, t0:t0 + 128, :], ["gof"], ["fb%d" % i])
        s.dma("sp", "ld", zb[i], s.d_zs[:, :, t0:t0 + 128].rearrange("h d t -> d h t"), ["zs"], ["fz%d" % i])
        s.tt("dve", ofb[i], ofb[i], obb[i], ALU.add, ["fa%d" % i, "fb%d" % i], ["fa%d" % i])
        s.act(sqb[i], ofb[i], AF.Square, ["fa%d" % i], ["fs%d" % i])
        s.S.emit("dve", "dve", (lambda e, o=ssb[i], a=v3(sqb[i], H): e.tensor_reduce(out=o, in_=a, axis=AX.X, op=ALU.add)),
                 ["fs%d" % i], ["fss%d" % i], False)
        s.act(ssb[i], ssb[i], AF.Sqrt, ["fss%d" % i, "kcol"], ["fss%d" % i], bias=s.kcol[:, 0:1], scale=1.0 / 128)
        s.recip(ssb[i], ssb[i], ["fss%d" % i], ["fss%d" % i])
        s.tt("dve", v3(ofb[i], H), v3(ofb[i], H), bc(ssb[i], 128), ALU.mult, ["fa%d" % i, "fss%d" % i], ["fa%d" % i])
        for bb in range((H + 3) // 4):
            b = rF(); pk = "ps%d" % b
            hs = list(range(bb * 4, min(H, bb * 4 + 4)))
            for h in hs:
                s.S.emit("pe", "pe", (lambda e, o=s.ps[b][:, (h % 4) * 128:(h % 4 + 1) * 128], a=ofb[i][:, h * 128:(h + 1) * 128]:
                                      e.transpose(out=o, in_=a, identity=s.ident)), ["fa%d" % i, "cf"], [pk], False)
            nh = len(hs)
            s.stt(outb[i][:, bb * 4:bb * 4 + nh, :], v3(s.ps[b][:, 0:nh * 128], nh), s.vec[:, c.v_bnorm:c.v_bnorm + 1],
                  zb[i][:, bb * 4:bb * 4 + nh, :], ALU.mult, ALU.mult, [pk, "vec", "fz%d" % i], ["fo%d" % i])
        s.dma("act", "st", s.d_oT[1, :, :, t0:t0 + 128].rearrange("h d t -> d h t"), outb[i], ["fo%d" % i], ["oT"])
    s.reset()


def host_consts(c):
    H, G = c.H, c.G
    ident = np.eye(128, dtype=np.float32)
    ones = np.ones((128, 128), np.float32)
    perm = np.zeros((128, 128), np.float32)
    for m in range(128):
        half = (m % 64) // 32
        k = m + 32 if half == 0 else m - 32
        perm[k, m] = 1.0
    cf32 = np.concatenate([ident, ones, perm], axis=1)
    j = np.arange(128)[:, None]; i = np.arange(128)[None, :]
    mprev = (j >= i).astype(np.float32); mnext = (j <= i).astype(np.float32)
    maskA = np.concatenate([np.tile(mprev, (1, G)), np.tile(mnext, (1, G))], axis=1).astype(ml_dtypes.bfloat16)
    t = np.arange(c.SEQ)
    row = (t // c.GW).astype(np.float32); col = (t % c.GW).astype(np.float32)
    inv = (1.0 / (np.float32(10000.0) ** (np.arange(0, 64, 2, dtype=np.float32) / np.float32(64)))).astype(np.float32)
    rope = np.zeros((2, 128, c.SEQ), np.float32)
    for p in range(128):
        axis = p // 64; half = (p % 64) // 32; f = p % 32
        ang = (row if axis == 0 else col) * inv[f]
        rope[0, p] = np.cos(ang)
        rope[1, p] = np.sin(ang) * (-1.0 if half == 0 else 1.0)
    jj = np.arange(64)[:, None]; cc = np.arange(64)[None, :]
    tri_f = (jj <= cc).astype(np.float32); tri_b = (jj >= cc).astype(np.float32)
    irep = np.tile(np.eye(64, dtype=np.float32), (1, H))
    neg_f = np.tile(np.where(cc <= jj, 0.0, -BIG).astype(np.float32), (1, H))
    neg_b = np.tile(np.where(cc >= jj, 0.0, -BIG).astype(np.float32), (1, H))
    str_f = np.tile((cc < jj).astype(np.float32), (1, H))
    str_b = np.tile((cc > jj).astype(np.float32), (1, H))
    g64 = np.concatenate([tri_f, tri_b, irep, neg_f, neg_b, str_f, str_b], axis=1)
    return dict(cf32=cf32, maskA=maskA, rope=rope, g64=g64)


def pack_cols(v):
    v = np.asarray(v, np.float32)
    lead = v.shape[:-1]
    K = v.shape[-1] // 128
    return np.moveaxis(v.reshape(lead + (K, 128)), -1, 0)


def host_layer_vecs(c, inp):
    L = c.L
    vecs = np.zeros((L, 128, c.NV), np.float32)
    rows = np.zeros((L, 128, c.NR), np.float32)
    H, KC = c.H, c.KC
    for l in range(L):
        vecs[l, :, c.v_npre:c.v_npre + 3 * KC] = pack_cols(inp["norm_pre"][l]).reshape(128, 3 * KC)
        vecs[l, :, c.v_npost:c.v_npost + 3 * KC] = pack_cols(inp["norm_post"][l]).reshape(128, 3 * KC)
        vecs[l, :, c.v_abias:c.v_abias + 9 * KC] = pack_cols(inp["ada_bias"][l].reshape(9, c.D)).reshape(128, 9 * KC)
        vecs[l, :, c.v_bconv:c.v_bconv + 9 * H] = pack_cols(inp["b_conv"][l]).reshape(128, 9 * H)
        vecs[l, :, c.v_bnorm] = inp["b_norm"][l]
        vecs[l, :, c.v_cq] = inp["c_qnorm"][l]
        vecs[l, :, c.v_ck] = inp["c_knorm"][l]
        r = np.concatenate([inp["a_sink"][l].reshape(-1), inp["b_A_log"][l].reshape(-1), inp["b_dt_bias"][l].reshape(-1)])
        rows[l] = np.broadcast_to(r[None, :], (128, c.NR))
    return vecs, rows


def prep_core(c, inp, b, shared):
    xT = np.ascontiguousarray(np.concatenate([inp["ctx"][b], inp["x"][b]], axis=0).T.astype(np.float32))
    cvec = np.stack([pack_cols(inp["c"][b]), pack_cols(inp["c_ctx"])], axis=-1).reshape(128, c.KC * 2)
    m = dict(shared)
    m["xT"] = xT
    m["cvec"] = np.ascontiguousarray(cvec.astype(np.float32))
    return m


def tile_w(W, tiles):
    W = np.asarray(W, np.float32)
    K, N = W.shape
    KC = K // 128
    out = np.empty((128, KC * N), np.float32)
    for (c0, w) in tiles:
        out[:, KC * c0:KC * (c0 + w)] = W[:, c0:c0 + w].reshape(KC, 128, w).transpose(1, 0, 2).reshape(128, KC * w)
    return out


def grp_tiles(c0, nchunks, gsz=2):
    t = []
    for i0 in range(0, nchunks, gsz):
        k = min(gsz, nchunks - i0)
        t.append((c0 + i0 * 128, k * 128))
    return t


def shared_inputs(c, inp):
    sh = host_consts(c)
    vecs, rows = host_layer_vecs(c, inp)
    sh["vecs"] = vecs; sh["rows"] = rows
    L, H, KV, KC, FC = c.L, c.H, c.KV, c.KC, c.FC
    for k in ("ada_down", "ada_up"):
        sh[k] = np.asarray(inp[k], np.float32)
    in_tiles = (grp_tiles(c.o_aq, H) + grp_tiles(c.o_ak, KV) + [(c.o_av, KV * 128)] + grp_tiles(c.o_bqkv, 3 * H)
                + grp_tiles(c.o_bz, H) + [(c.o_bg, 4 * H)] + grp_tiles(c.o_cq, H) + grp_tiles(c.o_ck, KV)
                + [(c.o_cv, KV * 128)] + grp_tiles(c.o_mg, 3 * KC))
    wgu = np.empty((L, 2, 128, KC * 2 * c.DFF), np.float32)
    wd = np.empty((L, 2, 128, FC * c.D), np.float32)
    win = np.empty((L, 128, KC * c.INW), np.float32)
    wbr = np.empty((L, 3, 128, H * c.D), np.float32)
    wout = np.empty((L, 128, KC * c.D), np.float32)
    for l in range(L):
        for i in range(2):
            W = np.asarray(inp["ffn_wgu"][l, i], np.float32)
            cat = np.concatenate([W[:, :c.DFF].reshape(c.D, FC, 128), W[:, c.DFF:].reshape(c.D, FC, 128)], axis=2).reshape(c.D, FC * 256)
            wgu[l, i] = tile_w(cat, [(j * 256, 256) for j in range(FC)])
            wd[l, i] = tile_w(inp["ffn_wd"][l, i], grp_tiles(0, KC))
        win[l] = tile_w(inp["w_in"][l], in_tiles)
        for i in range(3):
            wbr[l, i] = tile_w(inp["w_br"][l, i], [(fc * 128, 128) for fc in range(KC)])
        wout[l] = tile_w(inp["w_out"][l], grp_tiles(0, KC))
    sh["ffn_wgu"] = wgu; sh["ffn_wd"] = wd; sh["w_in"] = win; sh["w_br"] = wbr; sh["w_out"] = wout
    return sh


_CACHE = {}


def kernel(**inputs):
    c = Cfg()
    inp = {k: np.asarray(v) for k, v in inputs.items()}
    if "b" not in _CACHE:
        _CACHE["b"] = Builder(c)
    bld = _CACHE["b"]
    sh = shared_inputs(c, inp)
    B = inp["x"].shape[0]
    in_maps = [prep_core(c, inp, b, sh) for b in range(B)]
    res = run_bass_kernel_spmd(bld.nc, in_maps, core_ids=list(range(B)))
    out = np.stack([np.ascontiguousarray(res.results[b]["outT"].T) for b in range(B)], axis=0)
    return out.astype(np.float32)
```
